# Optimizing a Trainium2 kernel written in Bass

```python
import math
import jax, jax.numpy as jnp
from jax import lax
import numpy as np

D_MODEL = 1024
BATCH = 32
SEQ = 2048
DEPTH = 1
DEC_BATCH = 2
DEC_SEQ = 8192
PAST_LEN = 128

N_HEADS = 8
QK_HEAD_DIM = 64
V_HEAD_DIM = 2 * QK_HEAD_DIM
Q_WIDTH = N_HEADS * 2 * QK_HEAD_DIM
ATTN_WIDTH = N_HEADS * V_HEAD_DIM
N_FOURIER_GROUPS = 4
FOURIER_GROUP_DIM = 128
FOURIER_WIDTH = N_FOURIER_GROUPS * FOURIER_GROUP_DIM
ROPE_DIM = QK_HEAD_DIM // 4
ROPE_THETA = 500000.0
D_FF = 4 * D_MODEL
PLE_DIM = 256
Q_BLOCK = 128
EPS = 1e-6
LAMBDA_STD = 0.1
IN_WIDTH = FOURIER_WIDTH + Q_WIDTH + Q_WIDTH + ATTN_WIDTH + D_MODEL + D_MODEL

kernel_name = "hybrid_fnet_diffattn_encoder"


def rmsnorm(x, g):
    xf = x.astype(jnp.float32)
    y = xf * lax.rsqrt(jnp.mean(xf * xf, axis=-1, keepdims=True) + EPS)
    return (y * g.astype(jnp.float32)).astype(x.dtype)


def partial_rope(x):
    S = x.shape[1]
    half = ROPE_DIM // 2
    pos = jnp.arange(S, dtype=jnp.float32)
    inv_freq = ROPE_THETA ** (-(jnp.arange(0, ROPE_DIM, 2, dtype=jnp.float32) / ROPE_DIM))
    ang = pos[:, None] * inv_freq[None, :]
    cos = jnp.cos(ang)[None, :, None, None, :].astype(x.dtype)
    sin = jnp.sin(ang)[None, :, None, None, :].astype(x.dtype)
    x1 = x[..., :half]
    x2 = x[..., half:ROPE_DIM]
    rot = jnp.concatenate([x1 * cos - x2 * sin, x2 * cos + x1 * sin], axis=-1)
    return jnp.concatenate([rot, x[..., ROPE_DIM:]], axis=-1)


def fourier_mix(f):
    B, S, _ = f.shape
    fg = f.astype(jnp.float32).reshape(B, S, N_FOURIER_GROUPS, FOURIER_GROUP_DIM)
    out = jnp.fft.fft2(fg, axes=(1, 3), norm="ortho").real
    return out.reshape(B, S, FOURIER_WIDTH).astype(f.dtype)


def diff_attention(q, k, v, lam):
    B, S, H, _, Dh = q.shape
    nb = S // Q_BLOCK
    scale = 1.0 / math.sqrt(Dh)
    qb = q.reshape(B, nb, Q_BLOCK, H, 2, Dh).transpose(1, 0, 2, 3, 4, 5)

    def block(qi):
        s = jnp.einsum('bqhcd,bkhcd->bchqk', qi, k, preferred_element_type=jnp.float32) * scale
        pr = jax.nn.softmax(s, axis=-1)
        a = pr[:, 0] - lam * pr[:, 1]
        return jnp.einsum('bhqk,bkhd->bqhd', a.astype(v.dtype), v)

    out = lax.map(block, qb)
    return out.transpose(1, 0, 2, 3, 4).reshape(B, S, H, V_HEAD_DIM)


def layer(x, p, layer_idx, norm_mix_pre, w_in, w_fourier, w_attn, w_out,
          lambda_q1, lambda_k1, lambda_q2, lambda_k2, subln,
          norm_mix_post, norm_mlp_pre, w_up, w_down, norm_mlp_post,
          w_ple, w_ple_gate, norm_ple_post):
    B, S, _ = x.shape
    h = rmsnorm(x, norm_mix_pre)
    proj = h @ w_in
    o = 0
    f_in = proj[..., o:o + FOURIER_WIDTH]; o += FOURIER_WIDTH
    q = proj[..., o:o + Q_WIDTH]; o += Q_WIDTH
    k = proj[..., o:o + Q_WIDTH]; o += Q_WIDTH
    v = proj[..., o:o + ATTN_WIDTH]; o += ATTN_WIDTH
    g_f = jax.nn.sigmoid(proj[..., o:o + D_MODEL]); o += D_MODEL
    g_a = jax.nn.sigmoid(proj[..., o:o + D_MODEL])

    fourier_out = fourier_mix(f_in) @ w_fourier

    q = partial_rope(q.reshape(B, S, N_HEADS, 2, QK_HEAD_DIM))
    k = partial_rope(k.reshape(B, S, N_HEADS, 2, QK_HEAD_DIM))
    v = v.reshape(B, S, N_HEADS, V_HEAD_DIM)
    lam_init = 0.8 - 0.6 * math.exp(-0.3 * layer_idx)
    lam = (jnp.exp(jnp.sum(lambda_q1.astype(jnp.float32) * lambda_k1.astype(jnp.float32)))
           - jnp.exp(jnp.sum(lambda_q2.astype(jnp.float32) * lambda_k2.astype(jnp.float32)))
           + lam_init)
    att = diff_attention(q, k, v, lam)
    att = rmsnorm(att, subln) * (1.0 - lam_init)
    attn_out = att.reshape(B, S, ATTN_WIDTH) @ w_attn

    merged = g_f * fourier_out + g_a * attn_out
    x = x + rmsnorm(merged @ w_out, norm_mix_post)

    h2 = rmsnorm(x, norm_mlp_pre)
    u = jnp.square(jax.nn.relu(h2 @ w_up))
    x = x + rmsnorm(u @ w_down, norm_mlp_post)

    e = (p @ w_ple) * jax.nn.sigmoid(x @ w_ple_gate)
    x = x + rmsnorm(e, norm_ple_post)
    return x


def setup_inputs(seed: int = 0) -> dict:
    key = jax.random.key(seed)
    ks = jax.random.split(key, 24)
    f32 = jnp.float32

    def nrm(k, shape, scale=1.0):
        return jax.random.normal(k, shape, dtype=f32) * scale

    def gain(k):
        return 1.0 + 0.01 * jax.random.normal(k, (DEPTH, D_MODEL), dtype=f32)

    return {
        "x_prompt": nrm(ks[0], (BATCH, SEQ, D_MODEL)),
        "x_sample": nrm(ks[1], (DEC_BATCH, DEC_SEQ, D_MODEL)),
        "p_prompt": nrm(ks[2], (DEPTH, BATCH, SEQ, PLE_DIM)),
        "p_sample": nrm(ks[3], (DEPTH, DEC_BATCH, DEC_SEQ, PLE_DIM)),
        "norm_mix_pre": gain(ks[4]),
        "w_in": nrm(ks[5], (DEPTH, D_MODEL, IN_WIDTH), D_MODEL ** -0.5),
        "w_fourier": nrm(ks[6], (DEPTH, FOURIER_WIDTH, D_MODEL), FOURIER_WIDTH ** -0.5),
        "w_attn": nrm(ks[7], (DEPTH, ATTN_WIDTH, D_MODEL), ATTN_WIDTH ** -0.5),
        "w_out": nrm(ks[8], (DEPTH, D_MODEL, D_MODEL), D_MODEL ** -0.5),
        "lambda_q1": nrm(ks[9], (DEPTH, QK_HEAD_DIM), LAMBDA_STD),
        "lambda_k1": nrm(ks[10], (DEPTH, QK_HEAD_DIM), LAMBDA_STD),
        "lambda_q2": nrm(ks[11], (DEPTH, QK_HEAD_DIM), LAMBDA_STD),
        "lambda_k2": nrm(ks[12], (DEPTH, QK_HEAD_DIM), LAMBDA_STD),
        "subln": 1.0 + 0.01 * nrm(ks[13], (DEPTH, V_HEAD_DIM)),
        "norm_mix_post": gain(ks[14]),
        "norm_mlp_pre": gain(ks[15]),
        "w_up": nrm(ks[16], (DEPTH, D_MODEL, D_FF), D_MODEL ** -0.5),
        "w_down": nrm(ks[17], (DEPTH, D_FF, D_MODEL), D_FF ** -0.5),
        "norm_mlp_post": gain(ks[18]),
        "w_ple": nrm(ks[19], (DEPTH, PLE_DIM, D_MODEL), PLE_DIM ** -0.5),
        "w_ple_gate": nrm(ks[20], (DEPTH, D_MODEL, D_MODEL), D_MODEL ** -0.5),
        "norm_ple_post": gain(ks[21]),
    }


def reference(x_prompt, x_sample, p_prompt, p_sample, norm_mix_pre, w_in, w_fourier, w_attn, w_out,
              lambda_q1, lambda_k1, lambda_q2, lambda_k2, subln, norm_mix_post, norm_mlp_pre,
              w_up, w_down, norm_mlp_post, w_ple, w_ple_gate, norm_ple_post):
    y_prompt = x_prompt
    y_sample = x_sample
    for i in range(DEPTH):
        params = (norm_mix_pre[i], w_in[i], w_fourier[i], w_attn[i], w_out[i],
                  lambda_q1[i], lambda_k1[i], lambda_q2[i], lambda_k2[i], subln[i],
                  norm_mix_post[i], norm_mlp_pre[i], w_up[i], w_down[i], norm_mlp_post[i],
                  w_ple[i], w_ple_gate[i], norm_ple_post[i])
        y_prompt = layer(y_prompt, p_prompt[i], i, *params)
        y_sample = layer(y_sample, p_sample[i], i, *params)
    return (y_prompt, y_sample)
```

```python
import math
import numpy as np
import ml_dtypes
import concourse.bass as bass
import concourse.mybir as mybir
from concourse.bass_utils import run_bass_kernel_spmd

F32 = mybir.dt.float32
BF16 = mybir.dt.bfloat16
AF = mybir.ActivationFunctionType
ALU = mybir.AluOpType
AX = mybir.AxisListType

D = 1024
NH = 8
DFF = 4096
PLE = 256
EPS = 1e-6
LAM_INIT = 0.2
ROPE_THETA = 500000.0


class Buf:
    def __init__(self, name):
        self.name = name
        self.w = None
        self.r = []
        self.dsem = None
        self.dcnt = 0
        self.dtok = None


class Prog:
    CE = ["pe", "act", "dve", "pool"]

    def __init__(self, nc):
        self.nc = nc
        self.streams = {e: [] for e in self.CE + ["sp"]}
        self.sem = {e: nc.alloc_semaphore("s_" + e) for e in self.CE}
        self.semids = set(id(s) for s in self.sem.values())
        self.cnt = {e: 0 for e in self.CE}
        self.waited = {e: {} for e in self.CE + ["sp"]}
        self.dbufs = []
        self.free_sems = {}
        self.nsem = 0

    def buf(self, name, dma=False, persist=False):
        b = Buf(name)
        if dma:
            b.persist = persist
            b.q = {}
            self.dbufs.append(b)
        return b

    def _qsem(self, b, q):
        if q not in b.q:
            fl = self.free_sems.setdefault(q, [])
            if fl:
                sem, cnt = fl.pop()
            else:
                sem = self.nc.alloc_semaphore("dsem%d" % self.nsem)
                self.nsem += 1
                cnt = 0
            b.q[q] = [sem, cnt, None]
        return b.q[q]

    def release(self):
        keep = []
        for b in self.dbufs:
            if b.persist:
                keep.append(b)
            else:
                for q, (sem, cnt, tok) in b.q.items():
                    self.free_sems.setdefault(q, []).append((sem, cnt))
                b.q = None
        self.dbufs = keep

    def _deps(self, eng, reads, writes, extra=()):
        toks = []
        for b in reads:
            if b.w is not None:
                toks.append(b.w)
        for b in writes:
            if b.w is not None:
                toks.append(b.w)
            toks.extend(b.r)
        toks.extend(extra)
        best = {}
        for (key, sem, val, teng) in toks:
            if teng == "pe" and eng == "pe":
                continue
            if val <= self.waited[eng].get(key, 0):
                continue
            if key not in best or best[key][1] < val:
                best[key] = (sem, val)
        waits = []
        for key, (sem, val) in best.items():
            self.waited[eng][key] = val
            waits.append((sem, val))
        return waits

    def op(self, eng, fn, reads=(), writes=()):
        waits = self._deps(eng, reads, writes)
        self.cnt[eng] += 1
        sem = self.sem[eng]
        tok = (id(sem), sem, self.cnt[eng], eng)
        self.streams[eng].append((waits, fn, (sem, 1, False)))
        for b in reads:
            b.r.append(tok)
        for b in writes:
            b.w = tok
            b.r = []
        return tok

    def dma(self, q, fn, ndma, owner, reads=(), writes=()):
        ent = self._qsem(owner, q)
        extra = [ent[2]] if ent[2] is not None else []
        waits = self._deps(q, reads, writes, extra)
        ent[1] += 16 * ndma
        sem = ent[0]
        tok = (id(sem), sem, ent[1], "dma")
        ent[2] = tok
        self.streams[q].append((waits, fn, (sem, 16, True)))
        for b in reads:
            b.r.append(tok)
        for b in writes:
            b.w = tok
            b.r = []
        return tok

    def barrier(self):
        toks = [(id(self.sem[e]), self.sem[e], self.cnt[e], e) for e in self.CE if self.cnt[e] > 0]
        toks += [ent[2] for b in self.dbufs for ent in b.q.values() if ent[2] is not None]
        for e in self.CE + ["sp"]:
            waits = []
            for (key, sem, val, teng) in toks:
                if teng == e:
                    continue
                if val <= self.waited[e].get(key, 0):
                    continue
                self.waited[e][key] = val
                waits.append((sem, val))
            if waits:
                self.streams[e].append((waits, None, None))

    def emit(self):
        nc = self.nc
        engs = {"pe": "tensor", "act": "scalar", "dve": "vector", "pool": "gpsimd", "sp": "sync"}
        with nc.Block() as block:
            for e, attr in engs.items():
                stream = self.streams[e]

                def body(eng, stream=stream):
                    for waits, fn, inc in stream:
                        for (sem, val) in waits:
                            eng.wait_ge(sem, val)
                        if fn is None:
                            continue
                        res = fn(eng)
                        sem, amt, every = inc
                        if every:
                            if not isinstance(res, (list, tuple)):
                                res = [res]
                            for r in res:
                                r.then_inc(sem, amt)
                        else:
                            if isinstance(res, (list, tuple)):
                                res = res[-1]
                            res.then_inc(sem, amt)

                getattr(block, attr)(body)


class Arena:
    def __init__(self, nc, nbytes):
        self.t = nc.alloc_sbuf_tensor("arena", [128, nbytes // 2], BF16)
        self.top = 0
        self.cap = nbytes

    def alloc(self, shape, dt):
        esz = 4 if dt == F32 else 2
        n = 1
        for s in shape[1:]:
            n *= s
        size = (n * esz + 31) // 32 * 32
        off = self.top
        self.top += size
        assert self.top <= self.cap, ("SBUF arena overflow", self.top)
        ap = self.t[:, off // 2:(off + n * esz) // 2]
        if dt != BF16:
            ap = ap.bitcast(dt)
        if len(shape) == 3:
            ap = ap.rearrange("p (a b) -> p a b", a=shape[1])
        return ap


class Ring:
    def __init__(self, P, arena, name, n, shape, dt):
        self.tiles = [arena.alloc(shape, dt) for _ in range(n)]
        self.bufs = [P.buf("%s%d" % (name, i), dma=True) for i in range(n)]
        self.tag = [None] * n
        self.i = 0
        self.n = n

    def next(self, tag=None):
        if tag is not None:
            for j in range(self.n):
                if self.tag[j] == tag:
                    return self.tiles[j], self.bufs[j], True
        j = self.i
        self.i = (self.i + 1) % self.n
        self.tag[j] = tag
        return self.tiles[j], self.bufs[j], False


WSPEC = {
    "wq2": (8, 8, 256), "wk2": (8, 8, 256), "wv": (2, 8, 512), "wf": (1, 8, 512),
    "wg": (8, 8, 256), "wfa": (8, 12, 128), "wout": (2, 8, 512), "wup": (8, 8, 512),
    "wdown": (8, 4, 1024), "wple": (1, 2, 1024), "wpg": (2, 8, 512),
}


def build(cfg):
    KSTOP = cfg.get('stop', '')
    NQ = cfg["NQ"]
    units = cfg["units"]
    SMAX = max(u["S"] for u in units)
    NU = len(units)
    NTOK = NU * NQ
    nc = bass.Bass("TRN2", target_bir_lowering=False)
    P = Prog(nc)

    def din(name, shape, dt=F32):
        return nc.dram_tensor(name, list(shape), dt, kind="ExternalInput").ap()

    def dscr(name, shape, dt=BF16):
        return nc.dram_tensor(name, list(shape), dt, kind="Internal").ap()

    dr = {}
    dfts = {}
    for u, un in enumerate(units):
        S = un["S"]
        dr["xc%d" % u] = din("xc%d" % u, [S, D])
        if not un["same"]:
            dr["xq%d" % u] = din("xq%d" % u, [NQ, D])
            dr["ropeq%d" % u] = din("ropeq%d" % u, [2, 128, NQ])
        dr["pq%d" % u] = din("pq%d" % u, [NQ, PLE])
        dr["y%d" % u] = nc.dram_tensor("y%d" % u, [NQ, D], F32, kind="ExternalOutput").ap()
        dn = un["dft"]
        if dn not in dfts:
            dfts[dn] = din(dn, [2, NQ // 512, S // 512, 128, 4, 512], BF16)
    ropec = din("ropec", [2, 128, SMAX])
    wsrc = {k: din(k, [v[0], 128, v[1], v[2]]) for k, v in WSPEC.items()}
    wb = {k: dscr("b_" + k, [v[0], 128, v[1], v[2]]) for k, v in WSPEC.items()}
    cdft_in = din("cdft", [128, 256])
    gpre_in = din("gpre", [128, 8])
    gmlp_in = din("gmlp", [128, 8])
    subln_in = din("subln", [128, 1])
    gpost_in = din("gpost", [3, D])
    lamv_in = din("lamv", [4, 64])
    kt_scr = dscr("kt_scr", [NH, 128, SMAX])
    v_scr = dscr("v_scr", [NH, SMAX // 128, 128, 128])
    g_scr = dscr("g_scr", [SMAX, 1024])
    x1_scr = dscr("x1_scr", [NTOK, D], F32)

    A = Arena(nc, 206 * 1024)
    ident = A.alloc([128, 128], BF16)
    identf = A.alloc([128, 128], F32)
    mh = A.alloc([128, 1], F32)
    gpre = A.alloc([128, 8], F32)
    gmlp = A.alloc([128, 8], F32)
    sub8 = A.alloc([128, 1], F32)
    nlam = A.alloc([128, 1], F32)
    gpost = A.alloc([128, 3, D], F32)
    cdft = A.alloc([128, 256], BF16)
    stat = A.alloc([128, 64], F32)
    Bconst = P.buf("const", dma=True, persist=True)
    Bstat = [P.buf("stat%d" % i) for i in range(16)]
    stat_i = [0]

    def stat4():
        i = stat_i[0]
        stat_i[0] = (i + 1) % 16
        return stat[:, 4 * i:4 * i + 4], Bstat[i]

    ps = nc.alloc_psum_tensor("ps", [128, 4096], F32)
    PB = [P.buf("pb%d" % i) for i in range(8)]

    def bank(i):
        return ps[:, i * 512:(i + 1) * 512]

    def bank_bf(i):
        return ps[:, i * 512:(i + 1) * 512].bitcast(BF16)

    rot = {"i": 0}

    def nextbank(pool=(0, 1, 2, 3, 4, 5, 6, 7)):
        i = pool[rot["i"] % len(pool)]
        rot["i"] += 1
        return i

    base_top = A.top

    def ld_const(e):
        r = [e.dma_start(out=gpre, in_=gpre_in), e.dma_start(out=gmlp, in_=gmlp_in),
             e.dma_start(out=sub8, in_=subln_in),
             e.dma_start(out=gpost[:, 0, :], in_=gpost_in[0:1, :].partition_broadcast(128)),
             e.dma_start(out=gpost[:, 1, :], in_=gpost_in[1:2, :].partition_broadcast(128)),
             e.dma_start(out=gpost[:, 2, :], in_=gpost_in[2:3, :].partition_broadcast(128))]
        return r
    P.dma("sp", ld_const, 6, Bconst, writes=[Bconst])
    Bcd = P.buf("cdft", dma=True, persist=True)
    P.dma("pool", lambda e: e.dma_start(out=cdft, in_=cdft_in), 1, Bcd, writes=[Bcd])
    Bid = P.buf("ident")

    Bmh = P.buf("mh")
    P.op("pool", lambda e: e.memset(mh, -0.5), writes=[Bmh])
    P.op("pool", lambda e: e.memset(identf, 0.0), writes=[Bid])
    P.op("pool", lambda e: e.affine_select(out=identf, in_=identf, pattern=[[-1, 128]], compare_op=ALU.not_equal,
                                           fill=1.0, base=0, channel_multiplier=1), reads=[Bid], writes=[Bid])
    P.op("dve", lambda e: e.tensor_copy(out=ident, in_=identf), reads=[Bid], writes=[Bid])
    P.op("dve", lambda e: e.tensor_scalar(out=sub8, in0=sub8, scalar1=1.0 - LAM_INIT, scalar2=None, op0=ALU.mult),
         reads=[Bconst], writes=[Bconst])
    m0 = A.top
    lamt = A.alloc([128, 4, 64], F32)
    lamp = A.alloc([128, 2, 64], F32)
    Blam = P.buf("lam", dma=True)
    P.dma("sp", lambda e: e.dma_start(out=lamt.rearrange("p a b -> p (a b)"),
                                      in_=lamv_in.rearrange("a b -> (a b)").partition_broadcast(128)),
          1, Blam, writes=[Blam])
    P.op("dve", lambda e: e.tensor_tensor(out=lamp[:, 0, :], in0=lamt[:, 0, :], in1=lamt[:, 1, :], op=ALU.mult),
         reads=[Blam], writes=[Blam])
    P.op("dve", lambda e: e.tensor_tensor(out=lamp[:, 1, :], in0=lamt[:, 2, :], in1=lamt[:, 3, :], op=ALU.mult),
         reads=[Blam], writes=[Blam])
    st0, Bst0 = stat4()
    P.op("dve", lambda e: e.reduce_sum(out=st0[:, 0:2], in_=lamp, axis=AX.X), reads=[Blam], writes=[Bst0])
    P.op("act", lambda e: e.activation(out=st0[:, 2:4], in_=st0[:, 0:2], func=AF.Exp), reads=[Bst0], writes=[Bst0])
    P.op("dve", lambda e: e.tensor_tensor(out=nlam, in0=st0[:, 3:4], in1=st0[:, 2:3], op=ALU.subtract),
         reads=[Bst0], writes=[Bconst])
    P.op("dve", lambda e: e.tensor_scalar(out=nlam, in0=nlam, scalar1=-LAM_INIT, scalar2=None, op0=ALU.add),
         reads=[Bconst], writes=[Bconst])

    WR = Ring(P, A, "wcv", 4, [128, 4096], BF16)
    Bw = {k: P.buf("wscr_" + k) for k in WSPEC}
    FIRST = ("wk2", "wv", "wf")
    for k in FIRST:
        nt, a, b = WSPEC[k]
        for t in range(nt):
            tl, tb, _ = WR.next()
            v = tl[:, 0:a * b]
            P.dma("pool", lambda e, v=v, k=k, t=t: e.dma_start(out=v, in_=wsrc[k][t].rearrange("p a b -> p (a b)")),
                  1, tb, writes=[tb])
            P.dma("sp", lambda e, v=v, k=k, t=t: e.dma_start(out=wb[k][t].rearrange("p a b -> p (a b)"), in_=v),
                  1, tb, reads=[tb], writes=[Bw[k]])
    P.barrier()
    P.release()
    A.top = m0

    hTq = A.alloc([128, 8, NQ], BF16)
    hq_top = A.top
    attT = A.alloc([128, 8, NQ], BF16)
    fmT = A.alloc([128, 4, NQ], BF16)
    BhTq = [P.buf("hTq%d" % i) for i in range(NQ // 512)]
    BattT = [P.buf("attT%d" % i) for i in range(NQ // 512)]
    BfmT = [P.buf("fmT%d" % i) for i in range(NQ // 512)]
    unit_top = A.top

    save_top = A.top
    bg_base = A.cap - 2 * 8192
    A.top = bg_base
    WR2 = Ring(P, A, "wcv2", 2, [128, 4096], BF16)
    A.top = save_top
    bg = []
    bgq = []
    for k, (nt, a, b) in WSPEC.items():
        if k in FIRST:
            continue
        for t in range(nt):
            bg.append((k, t, a * b))

    def bg_step():
        if bg:
            k, t, n = bg.pop(0)
            tl, tb, _ = WR2.next()
            v = tl[:, 0:n]
            P.dma("pool", lambda e, v=v, k=k, t=t: e.dma_start(out=v, in_=wsrc[k][t].rearrange("p a b -> p (a b)")),
                  1, tb, writes=[tb])
            bgq.append((k, t, v, tb))
        if len(bgq) > 1 or (not bg and bgq):
            k, t, v, tb = bgq.pop(0)
            P.dma("sp", lambda e, v=v, k=k, t=t: e.dma_start(out=wb[k][t].rearrange("p a b -> p (a b)"), in_=v),
                  1, tb, reads=[tb], writes=[Bw[k]])

    def bg_flush():
        while bg or bgq:
            bg_step()

    def wload(ring, k, t, cache=False):
        nt, a, b = WSPEC[k]
        tl, tb, hit = ring.next((k, t) if cache else None)
        v = tl[:, 0:a * b]
        if not hit:
            P.dma("sp", lambda e: e.dma_start(out=v, in_=wb[k][t].rearrange("p a b -> p (a b)")),
                  1, tb, reads=[Bw[k]], writes=[tb])
        return v.rearrange("p (a b) -> p a b", a=a), tb

    alt = {"i": 0}

    def evac_eng():
        alt["i"] += 1
        return "act" if alt["i"] % 2 else "dve"

    def copy_op(eng, out, in_, reads, writes, scale=None):
        if eng == "act":
            if scale is None:
                P.op("act", lambda e: e.activation(out=out, in_=in_, func=AF.Copy), reads=reads, writes=writes)
            else:
                P.op("act", lambda e: e.activation(out=out, in_=in_, func=AF.Copy, scale=scale), reads=reads, writes=writes)
        else:
            if scale is None:
                P.op("dve", lambda e: e.tensor_copy(out=out, in_=in_), reads=reads, writes=writes)
            else:
                P.op("dve", lambda e: e.tensor_scalar(out=out, in0=in_, scalar1=scale, scalar2=None, op0=ALU.mult),
                     reads=reads, writes=writes)

    def rstd_ops(st, Bst, extra_mul=None):
        P.op("pool", lambda e: e.tensor_scalar(out=st[:, 1:2], in0=st[:, 0:1], scalar1=EPS, scalar2=None, op0=ALU.add),
             reads=[Bst], writes=[Bst])
        P.op("pool", lambda e: e.tensor_tensor(out=st[:, 2:3], in0=st[:, 1:2], in1=mh, op=ALU.pow),
             reads=[Bst, Bmh], writes=[Bst])
        if extra_mul is not None:
            P.op("pool", lambda e: e.tensor_scalar(out=st[:, 2:3], in0=st[:, 2:3], scalar1=extra_mul, scalar2=None,
                                                   op0=ALU.mult), reads=[Bst], writes=[Bst])

    def norm_T_block(xt, Bx, gcol, dst, Bdst, xn4, Bxn4, junk, Bjunk, prenorm=True):
        for t in range(4):
            if prenorm:
                st, Bst = stat4()
                P.op("act", lambda e, t=t, st=st: e.activation(out=junk, in_=xt[:, t, :], func=AF.Square,
                                                               scale=1.0 / 32.0, accum_out=st[:, 0:1]),
                     reads=[Bx], writes=[Bjunk, Bst])
                rstd_ops(st, Bst)
                P.op("dve", lambda e, t=t, st=st: e.tensor_scalar(out=xn4[:, t, :], in0=xt[:, t, :], scalar1=st[:, 2:3],
                                                                  scalar2=None, op0=ALU.mult),
                     reads=[Bx, Bst], writes=[Bxn4[t]])
            else:
                P.op("dve", lambda e, t=t: e.tensor_copy(out=xn4[:, t, :], in_=xt[:, t, :]), reads=[Bx], writes=[Bxn4[t]])
        for kp in range(4):
            bi = nextbank()
            bv = bank_bf(bi)

            def tr(e, kp=kp, bv=bv):
                r = None
                for kk in range(2):
                    kc = kp * 2 + kk
                    for t in range(4):
                        r = e.transpose(out=bv[:, kk * 512 + t * 128: kk * 512 + (t + 1) * 128],
                                        in_=xn4[:, t, kc * 128:(kc + 1) * 128], identity=ident)
                return r
            P.op("pe", tr, reads=list(Bxn4) + [Bid], writes=[PB[bi]])
            for kk in range(2):
                kc = kp * 2 + kk
                copy_op(evac_eng(), dst[:, kc, :], bv[:, kk * 512:(kk + 1) * 512], [PB[bi], Bconst], [Bdst],
                        scale=(gcol[:, kc:kc + 1] if gcol is not None else None))

    def do_unit(u, un):
        S = un["S"]
        same = un["same"]
        xc = dr["xc%d" % u]
        xq = xc if same else dr["xq%d" % u]
        ropeq = ropec if same else dr["ropeq%d" % u]
        dft = dfts[un["dft"]]
        KCH = min(S, 2048)

        A.top = hq_top
        SB = min(S, NQ)
        nb = SB // 512
        XR = Ring(P, A, "xa", 2, [128, 4, 1024], F32)
        xn4s = [A.alloc([128, 4, 1024], BF16) for _ in range(2)]
        Bxn4s = [[P.buf("xn4_%d_%d" % (i, t)) for t in range(4)] for i in range(2)]
        xn4, Bxn4 = xn4s[0], Bxn4s[0]
        junk = A.alloc([128, 1024], BF16)
        Bjunk = P.buf("junk")
        W8 = Ring(P, A, "w8", 3, [128, 4096], BF16)
        W4 = Ring(P, A, "w4", 3, [128, 2048], BF16)
        MR = Ring(P, A, "mask", nb, [128, 2, 512], F32)
        ktst = [A.alloc([128, 512], BF16) for _ in range(3)]
        Bktst = [P.buf("ktst%d" % i, dma=True) for i in range(3)]
        tmpA = [A.alloc([128, 512], F32) for _ in range(2)]
        BtmpA = [P.buf("tmpA%d" % i) for i in range(2)]
        tmpB = [A.alloc([128, 512], F32) for _ in range(2)]
        BtmpB = [P.buf("tmpB%d" % i) for i in range(2)]
        vst = [A.alloc([128, 4, 1024], BF16) for _ in range(2)]
        Bvst = [P.buf("vst%d" % i, dma=True) for i in range(2)]
        fT = [A.alloc([128, 4, 512], BF16) for _ in range(2)]
        BfT = [P.buf("fT%d" % i) for i in range(2)]
        gst = [A.alloc([128, 1024], BF16) for _ in range(2)]
        Bgst = [P.buf("gst%d" % i, dma=True) for i in range(2)]
        Bkt = P.buf("kt_scr")
        Bv = P.buf("v_scr")
        Bg = P.buf("g_scr")
        ridx = [0]
        assert A.top <= bg_base, ("phase A arena overlaps background cast ring", A.top, bg_base)
        for sbk in range(S // SB):
            t0 = sbk * SB
            masks = []
            for b in range(nb):
                xt, Bx, _ = XR.next()
                P.dma("sp", lambda e, xt=xt, r0=t0 + b * 512: e.dma_start(
                    out=xt, in_=xc[r0:r0 + 512, :].rearrange("(t p) d -> p t d", p=128)), 1, Bx, writes=[Bx])
                mt, Bm, _ = MR.next()
                P.dma("sp", lambda e, mt=mt, r0=t0 + b * 512: [e.dma_start(out=mt[:, 0, :], in_=ropec[0, :, r0:r0 + 512]),
                                                               e.dma_start(out=mt[:, 1, :], in_=ropec[1, :, r0:r0 + 512])],
                      2, Bm, writes=[Bm])
                masks.append((mt, Bm))
                norm_T_block(xt, Bx, gpre, hTq[:, :, b * 512:(b + 1) * 512], BhTq[b], xn4s[b % 2], Bxn4s[b % 2], junk, Bjunk)
            for h in range(NH):
                w, bw = wload(W4, "wk2", h)
                for b in range(nb):
                    mt, Bm = masks[b]
                    ba = nextbank((0, 2, 4, 6))
                    bb = ba + 1

                    def mm(e, w=w, ba=ba, bb=bb, b=b):
                        r = None
                        for kc in range(8):
                            r = e.matmul(bank(ba), lhsT=w[:, kc, 0:128], rhs=hTq[:, kc, b * 512:(b + 1) * 512],
                                         start=(kc == 0), stop=(kc == 7))
                        for kc in range(8):
                            r = e.matmul(bank(bb), lhsT=w[:, kc, 128:256], rhs=hTq[:, kc, b * 512:(b + 1) * 512],
                                         start=(kc == 0), stop=(kc == 7))
                        return r
                    P.op("pe", mm, reads=[bw, BhTq[b]], writes=[PB[ba], PB[bb]])
                    i2 = ridx[0] % 2
                    i3 = ridx[0] % 3
                    ridx[0] += 1
                    ta, Bta, tb_, Btb = tmpA[i2], BtmpA[i2], tmpB[i2], BtmpB[i2]
                    dst, Bd = ktst[i3], Bktst[i3]
                    P.op("dve", lambda e, tb_=tb_, bb=bb, mt=mt: e.tensor_tensor(out=tb_, in0=bank(bb), in1=mt[:, 1, :], op=ALU.mult),
                         reads=[PB[bb], Bm], writes=[Btb])
                    P.op("dve", lambda e, ta=ta, ba=ba, mt=mt: e.tensor_tensor(out=ta, in0=bank(ba), in1=mt[:, 0, :], op=ALU.mult),
                         reads=[PB[ba], Bm], writes=[Bta])
                    P.op("dve", lambda e, ta=ta, tb_=tb_, dst=dst: e.tensor_tensor(out=dst, in0=ta, in1=tb_, op=ALU.add),
                         reads=[Bta, Btb], writes=[Bd])
                    P.dma("pool", lambda e, h=h, dst=dst, r0=t0 + b * 512: e.dma_start(out=kt_scr[h, :, r0:r0 + 512], in_=dst),
                          1, Bd, reads=[Bd], writes=[Bkt])
                    bg_step()
            wv = [wload(W8, "wv", i) for i in range(2)]
            for b in range(nb):
                vs = vst[b % 2]
                Bvs = Bvst[b % 2]
                for t in range(4):
                    for half in range(2):
                        w, bw = wv[half]
                        bi = nextbank()

                        def mm(e, w=w, bi=bi, c0=b * 512 + t * 128):
                            r = None
                            for kc in range(8):
                                r = e.matmul(bank(bi), lhsT=hTq[:, kc, c0:c0 + 128], rhs=w[:, kc, :],
                                             start=(kc == 0), stop=(kc == 7))
                            return r
                        P.op("pe", mm, reads=[bw, BhTq[b]], writes=[PB[bi]])
                        copy_op(evac_eng(), vs[:, t, half * 512:(half + 1) * 512], bank(bi), [PB[bi]], [Bvs])

                def vstore(e, vs=vs, tb=(t0 + b * 512) // 512):
                    r = []
                    for h in range(NH):
                        r.append(e.dma_start(out=v_scr[h, tb * 4:(tb + 1) * 4].rearrange("t p d -> p t d"),
                                             in_=vs[:, :, h * 128:(h + 1) * 128]))
                    return r
                P.dma("pool", vstore, NH, Bvs, reads=[Bvs], writes=[Bv])
                bg_step()
            wf, bwf = wload(W8, "wf", 0)
            for b in range(nb):
                fTb, BfTb = fT[b % 2], BfT[b % 2]
                for g in range(4):
                    bi = nextbank()

                    def mm(e, g=g, bi=bi, wf=wf, b=b):
                        r = None
                        for kc in range(8):
                            r = e.matmul(bank(bi), lhsT=wf[:, kc, g * 128:(g + 1) * 128], rhs=hTq[:, kc, b * 512:(b + 1) * 512],
                                         start=(kc == 0), stop=(kc == 7))
                        return r
                    P.op("pe", mm, reads=[bwf, BhTq[b]], writes=[PB[bi]])
                    copy_op(evac_eng(), fTb[:, g, :], bank(bi), [PB[bi]], [BfTb])
                for t in range(4):
                    b0 = nextbank((0, 2, 4, 6))
                    gs = gst[t % 2]
                    Bgs = Bgst[t % 2]

                    def mm(e, t=t, b0=b0, fTb=fTb):
                        r = None
                        for g in range(4):
                            r = e.matmul(ps[:, b0 * 512 + g * 256: b0 * 512 + (g + 1) * 256],
                                         lhsT=fTb[:, g, t * 128:(t + 1) * 128], rhs=cdft, start=True, stop=True)
                        return r
                    P.op("pe", mm, reads=[BfTb, Bcd], writes=[PB[b0], PB[b0 + 1]])
                    copy_op(evac_eng(), gs, ps[:, b0 * 512:b0 * 512 + 1024], [PB[b0], PB[b0 + 1]], [Bgs])
                    P.dma("pool", lambda e, gs=gs, r0=t0 + b * 512 + t * 128: e.dma_start(
                        out=g_scr[r0:r0 + 128, :], in_=gs), 1, Bgs, reads=[Bgs], writes=[Bg])
                    bg_step()
        bg_flush()
        P.barrier()

        if not same:
            for qb in range(NQ // 512):
                xt, Bx, _ = XR.next()
                P.dma("sp", lambda e, xt=xt, qb=qb: e.dma_start(
                    out=xt, in_=xq[qb * 512:(qb + 1) * 512, :].rearrange("(t p) d -> p t d", p=128)), 1, Bx, writes=[Bx])
                norm_T_block(xt, Bx, gpre, hTq[:, :, qb * 512:(qb + 1) * 512], BhTq[qb], xn4, Bxn4, junk, Bjunk)
            P.barrier()
        P.release()

        if KSTOP == 'A':
            return
        A.top = unit_top
        GR = Ring(P, A, "gr", 2, [128, 4, 1024], BF16)
        DR = Ring(P, A, "dr", 2, [128, 2, 2048], BF16)
        for sb in range(NQ // 512):
            nsg = S // 512
            for sg in range(nsg):
                gt, Bgt, _ = GR.next()
                P.dma("sp", lambda e, gt=gt, sg=sg: e.dma_start(
                    out=gt, in_=g_scr[sg * 512:(sg + 1) * 512, :].rearrange("(c p) f -> p c f", p=128)),
                    1, Bgt, reads=[Bg], writes=[Bgt])
                dt_, Bdt, _ = DR.next()
                P.dma("sp", lambda e, dt_=dt_, sb=sb, sg=sg: [
                    e.dma_start(out=dt_[:, 0, :], in_=dft[0, sb, sg].rearrange("p c f -> p (c f)")),
                    e.dma_start(out=dt_[:, 1, :], in_=dft[1, sb, sg].rearrange("p c f -> p (c f)"))],
                    2, Bdt, writes=[Bdt])

                def mm(e, gt=gt, dt_=dt_, sg=sg, nsg=nsg):
                    r = None
                    for g in range(4):
                        for sc in range(4):
                            for ri in range(2):
                                first = (sg == 0 and sc == 0 and ri == 0)
                                last = (sg == nsg - 1 and sc == 3 and ri == 1)
                                r = e.matmul(bank(g), lhsT=gt[:, sc, g * 256 + ri * 128: g * 256 + (ri + 1) * 128],
                                             rhs=dt_[:, ri, sc * 512:(sc + 1) * 512], start=first, stop=last)
                    return r
                P.op("pe", mm, reads=[Bgt, Bdt], writes=[PB[0], PB[1], PB[2], PB[3]])
            for g in range(4):
                copy_op(evac_eng(), fmT[:, g, sb * 512:(sb + 1) * 512], bank(g), [PB[g]], [BfmT[sb]])
        P.barrier()
        P.release()

        if KSTOP == 'F':
            return
        A.top = unit_top
        W4 = Ring(P, A, "w4b", 2, [128, 2048], BF16)
        MR = Ring(P, A, "maskq", 2, [128, 2, 256], F32)
        tmpA = [A.alloc([128, 256], F32) for _ in range(2)]
        BtmpA = [P.buf("tmpA%d" % i) for i in range(2)]
        tmpB = [A.alloc([128, 256], F32) for _ in range(2)]
        BtmpB = [P.buf("tmpB%d" % i) for i in range(2)]
        qT = [A.alloc([128, NQ], BF16) for _ in range(2)]
        BqT = [[P.buf("qT%d_%d" % (i, j)) for j in range(NQ // 512)] for i in range(2)]
        nkc = KCH // 128
        nck = S // KCH
        KR = Ring(P, A, "kr", 3, [128, KCH], BF16)
        VR = Ring(P, A, "vr", 3, [128, nkc, 130], BF16)
        for j in range(3):
            P.op("pool", lambda e, j=j: [e.memset(VR.tiles[j][:, :, 128:129], 1.0), e.memset(VR.tiles[j][:, :, 129:130], 0.0)],
                 writes=[VR.bufs[j]])
        PT = [A.alloc([128, 2, 512], BF16) for _ in range(3)]
        BPT = [P.buf("pt%d" % i) for i in range(3)]
        osb = [A.alloc([128, 8, 130], F32) for _ in range(2)]
        Bosb = [P.buf("osb%d" % i) for i in range(2)]
        fst = [A.alloc([128, 16], F32) for _ in range(2)]
        Bfst = [P.buf("fst%d" % i) for i in range(2)]
        mh4 = A.alloc([128, 4], F32)
        Bmh4 = P.buf("mh4")
        P.op("pool", lambda e: e.memset(mh4, -0.5), writes=[Bmh4])
        tq = A.alloc([128, 4, 128], F32)
        Btq = P.buf("tq")
        oq = A.alloc([128, 4, 128], F32)
        Boq = P.buf("oq")
        sq = A.alloc([128, 4, 128], F32)
        Bsq = P.buf("sq")
        attb = [A.alloc([128, 4, 128], BF16) for _ in range(2)]
        Battb = [P.buf("attb%d" % i) for i in range(2)]
        oslot = [(5, 0), (5, 130), (5, 260), (6, 0), (6, 130), (6, 260), (7, 0), (7, 130)]
        sc_pairs = [(0, 1), (2, 3)]
        stt = {"spi": 0, "pti": 0, "ridx": 0, "fin": 0}

        def emit_qpiece(hh, piece):
            c0 = piece * 256
            mt, Bm, _ = MR.next()
            P.dma("sp", lambda e, mt=mt, c0=c0: [e.dma_start(out=mt[:, 0, :], in_=ropeq[0, :, c0:c0 + 256]),
                                                 e.dma_start(out=mt[:, 1, :], in_=ropeq[1, :, c0:c0 + 256])],
                  2, Bm, writes=[Bm])
            w, bw = wload(W4, "wq2", hh, cache=True)

            def mm(e, w=w, c0=c0):
                r = None
                for kc in range(8):
                    r = e.matmul(ps[:, 2048:2304], lhsT=w[:, kc, 0:128], rhs=hTq[:, kc, c0:c0 + 256],
                                 start=(kc == 0), stop=(kc == 7))
                for kc in range(8):
                    r = e.matmul(ps[:, 2304:2560], lhsT=w[:, kc, 128:256], rhs=hTq[:, kc, c0:c0 + 256],
                                 start=False, stop=(kc == 7), skip_group_check=True)
                return r
            P.op("pe", mm, reads=[bw, BhTq[c0 // 512]], writes=[PB[4]])
            i2 = stt["ridx"] % 2
            stt["ridx"] += 1
            ta, Bta, tb_, Btb = tmpA[i2], BtmpA[i2], tmpB[i2], BtmpB[i2]
            P.op("dve", lambda e, tb_=tb_, mt=mt: e.tensor_tensor(out=tb_, in0=ps[:, 2304:2560], in1=mt[:, 1, :], op=ALU.mult),
                 reads=[PB[4], Bm], writes=[Btb])
            P.op("dve", lambda e, ta=ta, mt=mt: e.tensor_tensor(out=ta, in0=ps[:, 2048:2304], in1=mt[:, 0, :], op=ALU.mult),
                 reads=[PB[4], Bm], writes=[Bta])
            qdst = qT[hh % 2]
            P.op("dve", lambda e, ta=ta, tb_=tb_, qdst=qdst, c0=c0: e.tensor_tensor(out=qdst[:, c0:c0 + 256], in0=ta, in1=tb_, op=ALU.add),
                 reads=[Bta, Btb], writes=[BqT[hh % 2][c0 // 512]])

        def emit_transposes(item):
            ab, Bab, hh, qb = item
            bv = bank_bf(4)

            def tr(e, ab=ab, bv=bv):
                r = None
                for qs in range(4):
                    r = e.transpose(out=bv[:, qs * 128:(qs + 1) * 128], in_=ab[:, qs, :], identity=ident)
                return r
            P.op("pe", tr, reads=[Bab, Bid], writes=[PB[4]])
            copy_op("dve", attT[:, hh, qb * 512:(qb + 1) * 512], bv[:, 0:512], [PB[4], Bconst], [BattT[qb]], scale=sub8[:, 0:1])

        def emit_exp(item):
            b0, b1, vt, Bvt, kc, firstk = item
            pt = PT[stt["pti"] % 3]
            Bpt = BPT[stt["pti"] % 3]
            stt["pti"] += 1
            P.op("act", lambda e, pt=pt, b0=b0: e.activation(
                out=pt.rearrange("p a b -> p (a b)"), in_=ps[:, b0 * 512:b0 * 512 + 1024], func=AF.Exp, scale=0.125),
                reads=[PB[b0], PB[b1]], writes=[Bpt])
            return (pt, Bpt, vt, Bvt, kc, firstk)

        def emit_av(pitem):
            pt, Bpt, vt, Bvt, kc, firstk = pitem

            def avmm(e, pt=pt, vt=vt, kc=kc, firstk=firstk):
                r = None
                seen = set()
                for qs in range(4):
                    for c in range(2):
                        bk, off = oslot[qs * 2 + c]
                        st_ = firstk and (bk not in seen)
                        seen.add(bk)
                        r = e.matmul(ps[:, bk * 512 + off: bk * 512 + off + 130],
                                     lhsT=pt[:, c, qs * 128:(qs + 1) * 128], rhs=vt[:, kc, :],
                                     start=st_, stop=False, skip_group_check=True)
                return r
            P.op("pe", avmm, reads=[Bpt, Bvt], writes=[PB[5], PB[6], PB[7]])

        def getkv(hh, ck):
            kt, Bk, hit = KR.next((u, hh, ck))
            if not hit:
                P.dma("sp", lambda e, kt=kt, ck=ck, hh=hh: e.dma_start(out=kt, in_=kt_scr[hh, :, ck * KCH:(ck + 1) * KCH]),
                      1, Bk, reads=[Bkt], writes=[Bk])
            vt, Bvt, hit = VR.next((u, hh, ck))
            if not hit:
                P.dma("sp", lambda e, vt=vt, ck=ck, hh=hh: e.dma_start(
                    out=vt[:, :, 0:128], in_=v_scr[hh, ck * nkc:(ck + 1) * nkc].rearrange("c p d -> p c d")),
                    1, Bvt, reads=[Bv], writes=[Bvt])
            return kt, Bk, vt, Bvt

        def finalize(hh, qb):
            fi = stt["fin"] % 2
            stt["fin"] += 1
            ob, Bob, fs, Bfs, ab, Bab = osb[fi], Bosb[fi], fst[fi], Bfst[fi], attb[fi], Battb[fi]
            for bk, j0, n in ((5, 0, 3), (6, 3, 3), (7, 6, 2)):
                P.op("dve", lambda e, bk=bk, j0=j0, n=n, ob=ob: e.tensor_copy(
                    out=ob[:, j0:j0 + n, :], in_=ps[:, bk * 512: bk * 512 + n * 130].rearrange("p (a b) -> p a b", a=n)),
                    reads=[PB[bk]], writes=[Bob])
            ov = ob.rearrange("p (q c) d -> p q c d", c=2)
            P.op("dve", lambda e, ob=ob, fs=fs: e.reciprocal(out=fs[:, 0:8], in_=ob[:, :, 128]), reads=[Bob], writes=[Bfs])
            P.op("dve", lambda e, fs=fs: e.tensor_scalar(out=fs[:, 8:12], in0=fs[:, 1:8:2], scalar1=nlam[:, 0:1], scalar2=None, op0=ALU.mult),
                 reads=[Bfs, Bconst], writes=[Bfs])
            P.op("dve", lambda e, ov=ov, fs=fs: e.tensor_tensor(out=tq, in0=ov[:, :, 1, 0:128],
                                                                in1=fs[:, 8:12].unsqueeze(2).broadcast_to([128, 4, 128]), op=ALU.mult),
                 reads=[Bob, Bfs], writes=[Btq])
            P.op("dve", lambda e, ov=ov, fs=fs: e.tensor_tensor(out=oq, in0=ov[:, :, 0, 0:128],
                                                                in1=fs[:, 0:8:2].unsqueeze(2).broadcast_to([128, 4, 128]), op=ALU.mult),
                 reads=[Bob, Bfs], writes=[Boq])
            P.op("pool", lambda e: e.tensor_tensor(out=oq, in0=oq, in1=tq, op=ALU.add), reads=[Btq, Boq], writes=[Boq])
            P.op("pool", lambda e: e.tensor_tensor(out=sq, in0=oq, in1=oq, op=ALU.mult), reads=[Boq], writes=[Bsq])
            P.op("dve", lambda e, fs=fs: e.reduce_sum(out=fs[:, 12:16], in_=sq, axis=AX.X), reads=[Bsq], writes=[Bfs])
            P.op("pool", lambda e, fs=fs: e.tensor_scalar(out=fs[:, 12:16], in0=fs[:, 12:16], scalar1=1.0 / 128.0, scalar2=EPS,
                                                          op0=ALU.mult, op1=ALU.add), reads=[Bfs], writes=[Bfs])
            P.op("pool", lambda e, fs=fs: e.tensor_tensor(out=fs[:, 12:16], in0=fs[:, 12:16], in1=mh4, op=ALU.pow),
                 reads=[Bfs, Bmh4], writes=[Bfs])
            P.op("dve", lambda e, fs=fs, ab=ab: e.tensor_tensor(out=ab, in0=oq, in1=fs[:, 12:16].unsqueeze(2).broadcast_to([128, 4, 128]),
                                                                op=ALU.mult), reads=[Boq, Bfs], writes=[Bab])
            return (ab, Bab, hh, qb)

        npiece = NQ // 256
        for piece in range(npiece):
            emit_qpiece(0, piece)
        pending_tr = None
        inject_at = (5, 10)
        for h in range(NH):
            qTh = qT[h % 2]
            BqTh = BqT[h % 2]
            nextpieces = list(range(npiece)) if h + 1 < NH else []
            per_qb = (npiece + NQ // 512 - 1) // (NQ // 512)
            for qb in range(NQ // 512):
                todo = [nextpieces.pop(0) for _ in range(min(per_qb, len(nextpieces)))]
                its = [(ck, kc) for ck in range(nck) for kc in range(nkc)]
                nit = len(its)
                kvc = {}

                def emit_S(j, qb=qb, its=its, kvc=kvc):
                    ck, kc = its[j]
                    if ck not in kvc:
                        kvc[ck] = getkv(h, ck)
                        if ck + 1 < nck:
                            getkv(h, ck + 1)
                        elif qb + 1 < NQ // 512:
                            getkv(h, 0)
                        elif h + 1 < NH:
                            getkv(h + 1, 0)
                    kt, Bk, vt, Bvt = kvc[ck]
                    b0, b1 = sc_pairs[stt["spi"] % 2]
                    stt["spi"] += 1

                    def smm(e, kt=kt, kc=kc, b0=b0, b1=b1, qb=qb, qTh=qTh):
                        e.matmul(bank(b0), lhsT=kt[0:64, kc * 128:(kc + 1) * 128], rhs=qTh[0:64, qb * 512:(qb + 1) * 512],
                                 start=True, stop=True)
                        return e.matmul(bank(b1), lhsT=kt[64:128, kc * 128:(kc + 1) * 128],
                                        rhs=qTh[64:128, qb * 512:(qb + 1) * 512], start=True, stop=True)
                    P.op("pe", smm, reads=[Bk, BqTh[qb]], writes=[PB[b0], PB[b1]])
                    return (b0, b1, vt, Bvt, kc, j == 0)

                sitems = {}
                for j in range(min(2, nit)):
                    sitems[j] = emit_S(j)
                tr_at = min(8, nit - 1)
                q_at = (3, 11) if nit >= 14 else (1, nit - 1)
                for i in range(nit):
                    pitem = emit_exp(sitems.pop(i))
                    if i + 2 < nit:
                        sitems[i + 2] = emit_S(i + 2)
                    emit_av(pitem)
                    if i == tr_at and pending_tr is not None:
                        emit_transposes(pending_tr)
                        pending_tr = None
                    if i in q_at and todo:
                        emit_qpiece(h + 1, todo.pop(0))
                while todo:
                    emit_qpiece(h + 1, todo.pop(0))
                if pending_tr is not None:
                    emit_transposes(pending_tr)
                    pending_tr = None
                pending_tr = finalize(h, qb)
        if pending_tr is not None:
            emit_transposes(pending_tr)
        P.barrier()
        P.release()

        if KSTOP == 'ATT':
            return
        A.top = unit_top
        XR = Ring(P, A, "xt1", 2, [128, 4, 1024], F32)
        W4 = Ring(P, A, "w4c", 4, [128, 2048], BF16)
        W8 = Ring(P, A, "w8c", 2, [128, 4096], BF16)
        mT = A.alloc([128, 8, NQ], BF16)
        BmT = [P.buf("mT%d" % i) for i in range(NQ // 512)]
        tf = [A.alloc([128, 512], F32) for _ in range(2)]
        Btf = [P.buf("tf%d" % i) for i in range(2)]
        m1 = A.alloc([128, 512], F32)
        Bm1 = P.buf("m1")
        m2 = A.alloc([128, 512], F32)
        Bm2 = P.buf("m2")
        junkf = A.alloc([128, 1024], BF16)
        Bjunkf = P.buf("junkf")
        tmp1 = A.alloc([128, 1024], F32)
        Btmp1 = P.buf("tmp1")
        flip = 0
        for oc in range(8):
            wfa, bwfa = wload(W4, "wfa", oc)
            wg, bwg = wload(W4, "wg", oc)
            for qb in range(NQ // 512):
                cols = slice(qb * 512, (qb + 1) * 512)
                bf_, ba_, bzf, bza = (0, 1, 2, 3) if flip % 2 == 0 else (4, 5, 6, 7)
                flip += 1

                def mm(e, wfa=wfa, wg=wg, bf_=bf_, ba_=ba_, bzf=bzf, bza=bza, cols=cols):
                    r = None
                    for g in range(4):
                        r = e.matmul(bank(bf_), lhsT=wfa[:, g, :], rhs=fmT[:, g, cols], start=(g == 0), stop=(g == 3))
                    for hh in range(8):
                        r = e.matmul(bank(ba_), lhsT=wfa[:, 4 + hh, :], rhs=attT[:, hh, cols], start=(hh == 0), stop=(hh == 7))
                    for kc in range(8):
                        r = e.matmul(bank(bzf), lhsT=wg[:, kc, 0:128], rhs=hTq[:, kc, cols], start=(kc == 0), stop=(kc == 7))
                    for kc in range(8):
                        r = e.matmul(bank(bza), lhsT=wg[:, kc, 128:256], rhs=hTq[:, kc, cols], start=(kc == 0), stop=(kc == 7))
                    return r
                P.op("pe", mm, reads=[bwfa, bwg, BfmT[qb], BattT[qb], BhTq[qb]], writes=[PB[bf_], PB[ba_], PB[bzf], PB[bza]])
                P.op("act", lambda e, bzf=bzf: e.activation(out=tf[0], in_=bank(bzf), func=AF.Tanh, scale=0.5),
                     reads=[PB[bzf]], writes=[Btf[0]])
                P.op("act", lambda e, bza=bza: e.activation(out=tf[1], in_=bank(bza), func=AF.Tanh, scale=0.5),
                     reads=[PB[bza]], writes=[Btf[1]])
                P.op("dve", lambda e, bf_=bf_: e.scalar_tensor_tensor(out=m1, in0=tf[0], scalar=1.0, in1=bank(bf_),
                                                                      op0=ALU.add, op1=ALU.mult),
                     reads=[Btf[0], PB[bf_]], writes=[Bm1])
                P.op("dve", lambda e, ba_=ba_: e.scalar_tensor_tensor(out=m2, in0=tf[1], scalar=1.0, in1=bank(ba_),
                                                                      op0=ALU.add, op1=ALU.mult),
                     reads=[Btf[1], PB[ba_]], writes=[Bm2])
                P.op("dve", lambda e, oc=oc, cols=cols: e.tensor_tensor(out=mT[:, oc, cols], in0=m1, in1=m2, op=ALU.add),
                     reads=[Bm1, Bm2], writes=[BmT[qb]])
        wo = [wload(W8, "wout", i) for i in range(2)]
        for qb in range(NQ // 512):
            xt, Bx, _ = XR.next()
            P.dma("sp", lambda e, xt=xt, qb=qb: e.dma_start(
                out=xt, in_=xq[qb * 512:(qb + 1) * 512, :].rearrange("(t p) d -> p t d", p=128)), 1, Bx, writes=[Bx])
            for t in range(4):
                b0 = nextbank((0, 2, 4, 6))

                def mm(e, c0=qb * 512 + t * 128, b0=b0, wo=wo):
                    r = None
                    for half in range(2):
                        for oc in range(8):
                            r = e.matmul(bank(b0 + half), lhsT=mT[:, oc, c0:c0 + 128], rhs=wo[half][0][:, oc, :],
                                         start=(oc == 0), stop=(oc == 7))
                    return r
                P.op("pe", mm, reads=[BmT[qb], wo[0][1], wo[1][1]], writes=[PB[b0], PB[b0 + 1]])
                y = ps[:, b0 * 512: b0 * 512 + 1024]
                st, Bst = stat4()
                P.op("act", lambda e, y=y, st=st: e.activation(out=junkf, in_=y, func=AF.Square, scale=0.5 / 32.0,
                                                               accum_out=st[:, 0:1]), reads=[PB[b0], PB[b0 + 1]], writes=[Bjunkf, Bst])
                rstd_ops(st, Bst, extra_mul=0.5)
                P.op("dve", lambda e, y=y, st=st: e.scalar_tensor_tensor(out=tmp1, in0=y, scalar=st[:, 2:3], in1=gpost[:, 0, :],
                                                                         op0=ALU.mult, op1=ALU.mult),
                     reads=[PB[b0], PB[b0 + 1], Bst, Bconst], writes=[Btmp1])
                P.op("dve", lambda e, xt=xt, t=t: e.tensor_tensor(out=xt[:, t, :], in0=xt[:, t, :], in1=tmp1, op=ALU.add),
                     reads=[Btmp1, Bx], writes=[Bx])
            r0 = u * NQ + qb * 512
            Bx1 = P.buf("x1scr")
            un.setdefault("Bx1", []).append(Bx1)
            P.dma("act", lambda e, xt=xt, r0=r0: e.dma_start(out=x1_scr[r0:r0 + 512, :].rearrange("(t p) d -> p t d", p=128), in_=xt),
                  1, Bx, reads=[Bx], writes=[Bx1])
        P.barrier()
        P.release()

    for u_, un_ in enumerate(units):
        do_unit(u_, un_)

    if KSTOP:
        P.barrier()
        P.emit()
        return nc
    A.top = base_top
    wdn = A.alloc([128, 32, 1024], BF16)
    Bwdn = P.buf("wdn", dma=True)
    P.dma("sp", lambda e: [e.dma_start(out=wdn[:, 4 * t:4 * t + 4, :].rearrange("p a b -> p (a b)"),
                                       in_=wb["wdown"][t].rearrange("p a b -> p (a b)")) for t in range(8)],
          8, Bwdn, reads=[Bw["wdown"]], writes=[Bwdn])
    XR = Ring(P, A, "xm", 2, [128, 4, 1024], F32)
    PR = Ring(P, A, "pm", 1, [128, 4, 256], F32)
    W8 = Ring(P, A, "w8m", 3, [128, 4096], BF16)
    xn4 = A.alloc([128, 4, 1024], BF16)
    Bxn4 = [P.buf("xn4m_%d" % t) for t in range(4)]
    junk = A.alloc([128, 1024], BF16)
    Bjunk = P.buf("junkm")
    h2T = A.alloc([128, 8, 512], BF16)
    Bh2T = P.buf("h2T")
    x2T = h2T
    Bx2T = [Bh2T for t in range(4)]
    uT = A.alloc([128, 32, 512], BF16)
    BuT = [P.buf("uT%d" % i) for i in range(8)]
    rl = [A.alloc([128, 512], F32) for _ in range(2)]
    Brl = [P.buf("rl%d" % i) for i in range(2)]
    stg = [A.alloc([128, 1024], BF16) for _ in range(1)]
    Bstg = [P.buf("stg%d" % i) for i in range(1)]
    pbf = A.alloc([128, 4, 256], BF16)
    Bpbf = P.buf("pbf")
    pT = A.alloc([128, 2, 512], BF16)
    BpT = P.buf("pT")
    tg = A.alloc([128, 1024], F32)
    Btg = P.buf("tg")
    mst = {"rli": 0}
    blocks = [(u, qb) for u in range(NU) for qb in range(NQ // 512)]
    ADD_ENG = cfg.get("m_add_eng", "dve")

    def m_load(bi_):
        u, qb = blocks[bi_]
        r0 = u * NQ + qb * 512
        xt, Bx, _ = XR.next()
        P.dma("sp", lambda e, xt=xt, r0=r0: e.dma_start(
            out=xt, in_=x1_scr[r0:r0 + 512, :].rearrange("(t p) d -> p t d", p=128)), 1, Bx,
            reads=[units[u]["Bx1"][qb]], writes=[Bx])
        return xt, Bx

    def m_prenorm(xt, Bx):
        for t in range(4):
            st, Bst = stat4()
            P.op("act", lambda e, t=t, st=st: e.activation(out=junk, in_=xt[:, t, :], func=AF.Square,
                                                           scale=1.0 / 32.0, accum_out=st[:, 0:1]),
                 reads=[Bx], writes=[Bjunk, Bst])
            rstd_ops(st, Bst)
            P.op("dve", lambda e, t=t, st=st: e.tensor_scalar(out=xn4[:, t, :], in0=xt[:, t, :], scalar1=st[:, 2:3],
                                                              scalar2=None, op0=ALU.mult),
                 reads=[Bx, Bst], writes=[Bxn4[t]])

    def m_T():
        for kp in range(4):
            bi = nextbank((0, 1, 2, 3))
            bv = bank_bf(bi)

            def tr(e, kp=kp, bv=bv):
                r = None
                for kk in range(2):
                    kc = kp * 2 + kk
                    for t in range(4):
                        r = e.transpose(out=bv[:, kk * 512 + t * 128: kk * 512 + (t + 1) * 128],
                                        in_=xn4[:, t, kc * 128:(kc + 1) * 128], identity=ident)
                return r
            P.op("pe", tr, reads=list(Bxn4) + [Bid], writes=[PB[bi]])
            for kk in range(2):
                kc = kp * 2 + kk
                copy_op(evac_eng(), h2T[:, kc, :], bv[:, kk * 512:(kk + 1) * 512], [PB[bi], Bconst], [Bh2T],
                        scale=gmlp[:, kc:kc + 1])

    def m_up():
        for fg in range(8):
            w, bw = wload(W8, "wup", fg)
            for j in range(4):
                fc = fg * 4 + j
                bi = nextbank((0, 1, 2, 3))

                def mm(e, w=w, j=j, bi=bi):
                    r = None
                    for kc in range(8):
                        r = e.matmul(bank(bi), lhsT=w[:, kc, j * 128:(j + 1) * 128], rhs=h2T[:, kc, :],
                                     start=(kc == 0), stop=(kc == 7))
                    return r
                P.op("pe", mm, reads=[bw, Bh2T], writes=[PB[bi]])
                r_ = rl[mst["rli"] % 2]
                Br_ = Brl[mst["rli"] % 2]
                mst["rli"] += 1
                P.op("act", lambda e, r_=r_, bi=bi: e.activation(out=r_, in_=bank(bi), func=AF.Relu), reads=[PB[bi]], writes=[Br_])
                P.op("dve", lambda e, r_=r_, fc=fc: e.scalar_tensor_tensor(out=uT[:, fc, :], in0=r_, scalar=1.0, in1=r_,
                                                                           op0=ALU.mult, op1=ALU.mult),
                     reads=[Br_], writes=[BuT[fg]])

    def m_down(xt, Bx):
        for t in range(4):
            b0 = nextbank((4, 6))

            def mm(e, t=t, b0=b0):
                r = None
                for half in range(2):
                    for fc in range(32):
                        r = e.matmul(bank(b0 + half), lhsT=uT[:, fc, t * 128:(t + 1) * 128],
                                     rhs=wdn[:, fc, half * 512:(half + 1) * 512], start=(fc == 0), stop=(fc == 31))
                return r
            P.op("pe", mm, reads=list(BuT) + [Bwdn], writes=[PB[b0], PB[b0 + 1]])
            y = ps[:, b0 * 512: b0 * 512 + 1024]
            st, Bst = stat4()
            P.op("act", lambda e, y=y, st=st: e.activation(out=junk, in_=y, func=AF.Square, scale=1.0 / 32.0,
                                                           accum_out=st[:, 0:1]), reads=[PB[b0], PB[b0 + 1]], writes=[Bjunk, Bst])
            rstd_ops(st, Bst)
            P.op("dve", lambda e, y=y, st=st: e.scalar_tensor_tensor(out=tg, in0=y, scalar=st[:, 2:3], in1=gpost[:, 1, :],
                                                                     op0=ALU.mult, op1=ALU.mult),
                 reads=[PB[b0], PB[b0 + 1], Bst, Bconst], writes=[Btg])
            P.op(ADD_ENG, lambda e, xt=xt, t=t: e.tensor_tensor(out=xt[:, t, :], in0=xt[:, t, :], in1=tg, op=ALU.add),
                 reads=[Btg, Bx], writes=[Bx])

    def m_ple(bi_, xt, Bx):
        u, qb = blocks[bi_]
        ptile, Bp, _ = PR.next()
        P.dma("sp", lambda e, ptile=ptile, u=u, qb=qb: e.dma_start(
            out=ptile, in_=dr["pq%d" % u][qb * 512:(qb + 1) * 512, :].rearrange("(t p) d -> p t d", p=128)), 1, Bp, writes=[Bp])
        for t in range(4):
            sg, Bsg = stg[0], Bstg[0]
            P.op("dve", lambda e, t=t, sg=sg: e.tensor_copy(out=sg, in_=xt[:, t, :]), reads=[Bx], writes=[Bsg])
            bi = nextbank((0, 1, 2, 3))
            bv = bank_bf(bi)

            def tr(e, sg=sg, bv=bv):
                r = None
                for kc in range(8):
                    r = e.transpose(out=bv[:, kc * 128:(kc + 1) * 128], in_=sg[:, kc * 128:(kc + 1) * 128], identity=ident)
                return r
            P.op("pe", tr, reads=[Bsg, Bid], writes=[PB[bi]])
            copy_op(evac_eng(), x2T[:, :, t * 128:(t + 1) * 128], bv.rearrange("p (a b) -> p a b", a=8), [PB[bi]], [Bx2T[t]])
        P.op("dve", lambda e, ptile=ptile: e.tensor_copy(out=pbf, in_=ptile), reads=[Bp], writes=[Bpbf])
        bi = nextbank((0, 1, 2, 3))
        bv = bank_bf(bi)

        def trp(e, bv=bv):
            r = None
            for c in range(2):
                for t in range(4):
                    r = e.transpose(out=bv[:, c * 512 + t * 128: c * 512 + (t + 1) * 128],
                                    in_=pbf[:, t, c * 128:(c + 1) * 128], identity=ident)
            return r
        P.op("pe", trp, reads=[Bpbf, Bid], writes=[PB[bi]])
        copy_op("dve", pT.rearrange("p a b -> p (a b)"), bv, [PB[bi]], [BpT])
        wpgt = [wload(W8, "wpg", i) for i in range(2)]
        wple, Bwp = wload(W8, "wple", 0) if False else (None, None)
        for t in range(4):
            be = nextbank((0, 2))
            bg_ = nextbank((4, 6))

            def mm(e, t=t, be=be, bg_=bg_, wpgt=wpgt):
                r = None
                for half in range(2):
                    for c in range(2):
                        r = e.matmul(bank(be + half), lhsT=pT[:, c, t * 128:(t + 1) * 128],
                                     rhs=wple_r[:, c, half * 512:(half + 1) * 512], start=(c == 0), stop=(c == 1))
                for half in range(2):
                    for kc in range(8):
                        r = e.matmul(bank(bg_ + half), lhsT=x2T[:, kc, t * 128:(t + 1) * 128],
                                     rhs=wpgt[half][0][:, kc, :], start=(kc == 0), stop=(kc == 7))
                return r
            P.op("pe", mm, reads=[BpT, Bx2T[t], Bwple, wpgt[0][1], wpgt[1][1]], writes=[PB[be], PB[be + 1], PB[bg_], PB[bg_ + 1]])
            ye = ps[:, be * 512: be * 512 + 1024]
            yg = ps[:, bg_ * 512: bg_ * 512 + 1024]
            P.op("act", lambda e, yg=yg: e.activation(out=tg, in_=yg, func=AF.Tanh, scale=0.5),
                 reads=[PB[bg_], PB[bg_ + 1]], writes=[Btg])
            P.op("dve", lambda e, ye=ye: e.scalar_tensor_tensor(out=tg, in0=tg, scalar=1.0, in1=ye, op0=ALU.add, op1=ALU.mult),
                 reads=[PB[be], PB[be + 1]], writes=[Btg])
            st, Bst = stat4()
            P.op("act", lambda e, st=st: e.activation(out=junk, in_=tg, func=AF.Square, scale=0.5 / 32.0,
                                                      accum_out=st[:, 0:1]), reads=[Btg], writes=[Bjunk, Bst])
            rstd_ops(st, Bst, extra_mul=0.5)
            P.op("dve", lambda e, st=st: e.scalar_tensor_tensor(out=tg, in0=tg, scalar=st[:, 2:3], in1=gpost[:, 2, :],
                                                                op0=ALU.mult, op1=ALU.mult),
                 reads=[Bst, Bconst], writes=[Btg])
            P.op(ADD_ENG, lambda e, xt=xt, t=t: e.tensor_tensor(out=xt[:, t, :], in0=xt[:, t, :], in1=tg, op=ALU.add),
                 reads=[Btg, Bx], writes=[Bx])
        P.dma("act", lambda e, xt=xt, u=u, qb=qb: e.dma_start(
            out=dr["y%d" % u][qb * 512:(qb + 1) * 512, :].rearrange("(t p) d -> p t d", p=128), in_=xt),
            1, Bx, reads=[Bx])

    wple_r = A.alloc([128, 2, 1024], BF16)
    Bwple = P.buf("wple_r", dma=True)
    P.dma("sp", lambda e: e.dma_start(out=wple_r.rearrange("p a b -> p (a b)"), in_=wb["wple"][0].rearrange("p a b -> p (a b)")),
          1, Bwple, reads=[Bw["wple"]], writes=[Bwple])
    nblk = len(blocks)
    cur = m_load(0)
    m_prenorm(cur[0], cur[1])
    m_T()
    m_up()
    for bi_ in range(nblk):
        nxt = None
        if bi_ + 1 < nblk:
            nxt = m_load(bi_ + 1)
            m_prenorm(nxt[0], nxt[1])
        m_down(cur[0], cur[1])
        if nxt is not None:
            m_T()
            m_up()
        m_ple(bi_, *cur)
        cur = nxt
    P.barrier()
    P.emit()
    return nc


def _tile_w(W, cols):
    K, N = W.shape
    return np.ascontiguousarray(W.reshape(K // 128, 128, N // cols, cols).transpose(2, 1, 0, 3))


def _swap_cols(W):
    K, N = W.shape
    Wr = W.reshape(K, N // 64, 64).copy()
    out = Wr.copy()
    out[:, :, 0:8] = Wr[:, :, 8:16]
    out[:, :, 8:16] = Wr[:, :, 0:8]
    return out.reshape(K, N)


def _pair_tiles(Wa, Wb):
    ta = _tile_w(Wa, 128)
    tb = _tile_w(Wb, 128)
    return np.ascontiguousarray(np.concatenate([ta, tb], axis=3))


def prep_weights(w_in, w_fourier, w_attn, w_out, w_up, w_down, w_ple, w_ple_gate):
    o = 0
    wf = w_in[:, o:o + 512]; o += 512
    wq = w_in[:, o:o + 1024]; o += 1024
    wk = w_in[:, o:o + 1024]; o += 1024
    wv = w_in[:, o:o + 1024]; o += 1024
    wgf = w_in[:, o:o + 1024]; o += 1024
    wga = w_in[:, o:o + 1024]
    d = {}
    d["wq2"] = _pair_tiles(wq, _swap_cols(wq))
    d["wk2"] = _pair_tiles(wk, _swap_cols(wk))
    d["wv"] = _tile_w(wv, 512)
    d["wf"] = _tile_w(wf, 512)
    d["wg"] = _pair_tiles(wgf, wga)
    tf = _tile_w(w_fourier, 128)
    ta = _tile_w(w_attn, 128)
    d["wfa"] = np.ascontiguousarray(np.concatenate([tf, ta], axis=2))
    d["wout"] = _tile_w(w_out, 512)
    d["wup"] = _tile_w(w_up, 512)
    d["wdown"] = np.ascontiguousarray(w_down.reshape(8, 4, 128, 1024).transpose(0, 2, 1, 3))
    d["wple"] = np.ascontiguousarray(w_ple.reshape(1, 2, 128, 1024).transpose(0, 2, 1, 3))
    d["wpg"] = _tile_w(w_ple_gate, 512)
    return {k: v.astype(np.float32) for k, v in d.items()}


def rope_tables(pos):
    pos = np.asarray(pos, dtype=np.float32)
    inv_freq = (np.float32(ROPE_THETA) ** (-(np.arange(0, 16, 2, dtype=np.float32) / np.float32(16)))).astype(np.float32)
    ang = (pos[:, None] * inv_freq[None, :]).astype(np.float32)
    cos = np.cos(ang).astype(np.float32).T
    sin = np.sin(ang).astype(np.float32).T
    n = len(pos)
    cm = np.ones((128, n), np.float32)
    sm = np.zeros((128, n), np.float32)
    for c in range(2):
        cm[c * 64:c * 64 + 8] = cos
        cm[c * 64 + 8:c * 64 + 16] = cos
        sm[c * 64:c * 64 + 8] = -sin
        sm[c * 64 + 8:c * 64 + 16] = sin
    return np.stack([cm, sm])


def dft_tables(S, spos):
    spos = np.asarray(spos, dtype=np.int64)
    nq = len(spos)
    s = np.arange(S, dtype=np.int64)
    m = (s[:, None] * spos[None, :]) % S
    ang = m.astype(np.float64) * (2.0 * np.pi / S)
    sc = 1.0 / math.sqrt(128.0 * S)
    out = np.empty((2, nq // 512, S // 512, 128, 4, 512), dtype=ml_dtypes.bfloat16)
    for ri, f in enumerate((np.cos, np.sin)):
        t = (f(ang) * sc).astype(np.float32)
        t = t.reshape(S // 512, 4, 128, nq // 512, 512)
        out[ri] = t.transpose(3, 0, 2, 1, 4).astype(ml_dtypes.bfloat16)
    return out


def cdft_table():
    c = np.arange(128, dtype=np.int64)
    ang = ((c[:, None] * c[None, :]) % 128).astype(np.float64) * (2.0 * np.pi / 128)
    return np.concatenate([np.cos(ang), -np.sin(ang)], axis=1).astype(np.float32)


def small_inputs(norm_mix_pre, norm_mlp_pre, subln, norm_mix_post, norm_mlp_post, norm_ple_post,
                 lambda_q1, lambda_k1, lambda_q2, lambda_k2):
    d = {}
    d["gpre"] = np.ascontiguousarray(norm_mix_pre.reshape(8, 128).T).astype(np.float32)
    d["gmlp"] = np.ascontiguousarray(norm_mlp_pre.reshape(8, 128).T).astype(np.float32)
    d["subln"] = np.ascontiguousarray(subln.reshape(128, 1)).astype(np.float32)
    d["gpost"] = np.stack([norm_mix_post, norm_mlp_post, norm_ple_post]).astype(np.float32)
    d["lamv"] = np.stack([lambda_q1, lambda_k1, lambda_q2, lambda_k2]).astype(np.float32)
    d["cdft"] = cdft_table()
    return d


_CACHE = {}


def kernel(x_prompt, x_sample, p_prompt, p_sample, norm_mix_pre, w_in, w_fourier, w_attn, w_out,
           lambda_q1, lambda_k1, lambda_q2, lambda_k2, subln, norm_mix_post, norm_mlp_pre,
           w_up, w_down, norm_mlp_post, w_ple, w_ple_gate, norm_ple_post):
    f = lambda a: np.asarray(a, dtype=np.float32)
    x_prompt, x_sample, p_prompt, p_sample = f(x_prompt), f(x_sample), f(p_prompt), f(p_sample)
    NC = 8
    NQ = 2048
    SP, SS = 2048, 8192
    cfg = {"NQ": NQ, "units": [{"S": SP, "same": True, "dft": "dftp"} for _ in range(4)]
           + [{"S": SS, "same": False, "dft": "dfts"}]}
    nc = build(cfg)
    shared = prep_weights(f(w_in[0]), f(w_fourier[0]), f(w_attn[0]), f(w_out[0]), f(w_up[0]), f(w_down[0]),
                          f(w_ple[0]), f(w_ple_gate[0]))
    shared.update(small_inputs(f(norm_mix_pre[0]), f(norm_mlp_pre[0]), f(subln[0]), f(norm_mix_post[0]),
                               f(norm_mlp_post[0]), f(norm_ple_post[0]), f(lambda_q1[0]), f(lambda_k1[0]),
                               f(lambda_q2[0]), f(lambda_k2[0])))
    shared["ropec"] = rope_tables(np.arange(SS))
    shared["dftp"] = dft_tables(SP, np.arange(SP))
    in_maps = []
    for c in range(NC):
        m = dict(shared)
        for i in range(4):
            b = 4 * c + i
            m["xc%d" % i] = x_prompt[b]
            m["pq%d" % i] = p_prompt[0, b]
        sb, j = c // 4, c % 4
        qpos = np.arange(j * NQ, (j + 1) * NQ)
        m["xc4"] = x_sample[sb]
        m["xq4"] = np.ascontiguousarray(x_sample[sb, j * NQ:(j + 1) * NQ])
        m["pq4"] = np.ascontiguousarray(p_sample[0, sb, j * NQ:(j + 1) * NQ])
        m["ropeq4"] = rope_tables(qpos)
        m["dfts"] = dft_tables(SS, qpos)
        in_maps.append(m)
    res = run_bass_kernel_spmd(nc, in_maps, core_ids=list(range(NC)))
    y_prompt = np.empty((32, SP, D), np.float32)
    y_sample = np.empty((2, SS, D), np.float32)
    for c in range(NC):
        r = res.results[c]
        for i in range(4):
            y_prompt[4 * c + i] = r["y%d" % i]
        sb, j = c // 4, c % 4
        y_sample[sb, j * NQ:(j + 1) * NQ] = r["y4"]
    return (y_prompt, y_sample)
```

```python
import math
import numpy as np
import ml_dtypes
import concourse.bass as bass
import concourse.mybir as mybir
from concourse.bass_utils import run_bass_kernel_spmd

F32 = mybir.dt.float32
BF16 = mybir.dt.bfloat16
AF = mybir.ActivationFunctionType
ALU = mybir.AluOpType
AX = mybir.AxisListType

D = 1024
NH = 8
DFF = 4096
PLE = 256
EPS = 1e-6
LAM_INIT = 0.2
ROPE_THETA = 500000.0


class Buf:
    def __init__(self, name):
        self.name = name
        self.w = None
        self.r = []
        self.dsem = None
        self.dcnt = 0
        self.dtok = None


class Prog:
    CE = ["pe", "act", "dve", "pool"]

    def __init__(self, nc):
        self.nc = nc
        self.streams = {e: [] for e in self.CE + ["sp"]}
        self.sem = {e: nc.alloc_semaphore("s_" + e) for e in self.CE}
        self.semids = set(id(s) for s in self.sem.values())
        self.cnt = {e: 0 for e in self.CE}
        self.waited = {e: {} for e in self.CE + ["sp"]}
        self.dbufs = []
        self.free_sems = {}
        self.nsem = 0

    def buf(self, name, dma=False, persist=False):
        b = Buf(name)
        if dma:
            b.persist = persist
            b.q = {}
            self.dbufs.append(b)
        return b

    def _qsem(self, b, q):
        if q not in b.q:
            fl = self.free_sems.setdefault(q, [])
            if fl:
                sem, cnt = fl.pop()
            else:
                sem = self.nc.alloc_semaphore("dsem%d" % self.nsem)
                self.nsem += 1
                cnt = 0
            b.q[q] = [sem, cnt, None]
        return b.q[q]

    def release(self):
        keep = []
        for b in self.dbufs:
            if b.persist:
                keep.append(b)
            else:
                for q, (sem, cnt, tok) in b.q.items():
                    self.free_sems.setdefault(q, []).append((sem, cnt))
                b.q = None
        self.dbufs = keep

    def _deps(self, eng, reads, writes, extra=()):
        toks = []
        for b in reads:
            if b.w is not None:
                toks.append(b.w)
        for b in writes:
            if b.w is not None:
                toks.append(b.w)
            toks.extend(b.r)
        toks.extend(extra)
        best = {}
        for (key, sem, val, teng) in toks:
            if teng == "pe" and eng == "pe":
                continue
            if val <= self.waited[eng].get(key, 0):
                continue
            if key not in best or best[key][1] < val:
                best[key] = (sem, val)
        waits = []
        for key, (sem, val) in best.items():
            self.waited[eng][key] = val
            waits.append((sem, val))
        return waits

    def op(self, eng, fn, reads=(), writes=()):
        waits = self._deps(eng, reads, writes)
        self.cnt[eng] += 1
        sem = self.sem[eng]
        tok = (id(sem), sem, self.cnt[eng], eng)
        self.streams[eng].append((waits, fn, (sem, 1, False)))
        for b in reads:
            b.r.append(tok)
        for b in writes:
            b.w = tok
            b.r = []
        return tok

    def dma(self, q, fn, ndma, owner, reads=(), writes=()):
        ent = self._qsem(owner, q)
        extra = [ent[2]] if ent[2] is not None else []
        waits = self._deps(q, reads, writes, extra)
        ent[1] += 16 * ndma
        sem = ent[0]
        tok = (id(sem), sem, ent[1], "dma")
        ent[2] = tok
        self.streams[q].append((waits, fn, (sem, 16, True)))
        for b in reads:
            b.r.append(tok)
        for b in writes:
            b.w = tok
            b.r = []
        return tok

    def barrier(self):
        toks = [(id(self.sem[e]), self.sem[e], self.cnt[e], e) for e in self.CE if self.cnt[e] > 0]
        toks += [ent[2] for b in self.dbufs for ent in b.q.values() if ent[2] is not None]
        for e in self.CE + ["sp"]:
            waits = []
            for (key, sem, val, teng) in toks:
                if teng == e:
                    continue
                if val <= self.waited[e].get(key, 0):
                    continue
                self.waited[e][key] = val
                waits.append((sem, val))
            if waits:
                self.streams[e].append((waits, None, None))

    def emit(self):
        nc = self.nc
        engs = {"pe": "tensor", "act": "scalar", "dve": "vector", "pool": "gpsimd", "sp": "sync"}
        with nc.Block() as block:
            for e, attr in engs.items():
                stream = self.streams[e]

                def body(eng, stream=stream):
                    for waits, fn, inc in stream:
                        for (sem, val) in waits:
                            eng.wait_ge(sem, val)
                        if fn is None:
                            continue
                        res = fn(eng)
                        sem, amt, every = inc
                        if every:
                            if not isinstance(res, (list, tuple)):
                                res = [res]
                            for r in res:
                                r.then_inc(sem, amt)
                        else:
                            if isinstance(res, (list, tuple)):
                                res = res[-1]
                            res.then_inc(sem, amt)

                getattr(block, attr)(body)


class Arena:
    def __init__(self, nc, nbytes):
        self.t = nc.alloc_sbuf_tensor("arena", [128, nbytes // 2], BF16)
        self.top = 0
        self.cap = nbytes

    def alloc(self, shape, dt):
        esz = 4 if dt == F32 else 2
        n = 1
        for s in shape[1:]:
            n *= s
        size = (n * esz + 31) // 32 * 32
        off = self.top
        self.top += size
        assert self.top <= self.cap, ("SBUF arena overflow", self.top)
        ap = self.t[:, off // 2:(off + n * esz) // 2]
        if dt != BF16:
            ap = ap.bitcast(dt)
        if len(shape) == 3:
            ap = ap.rearrange("p (a b) -> p a b", a=shape[1])
        return ap


class Ring:
    def __init__(self, P, arena, name, n, shape, dt):
        self.tiles = [arena.alloc(shape, dt) for _ in range(n)]
        self.bufs = [P.buf("%s%d" % (name, i), dma=True) for i in range(n)]
        self.tag = [None] * n
        self.i = 0
        self.n = n

    def next(self, tag=None):
        if tag is not None:
            for j in range(self.n):
                if self.tag[j] == tag:
                    return self.tiles[j], self.bufs[j], True
        j = self.i
        self.i = (self.i + 1) % self.n
        self.tag[j] = tag
        return self.tiles[j], self.bufs[j], False


WSPEC = {
    "wq2": (8, 8, 256), "wk2": (8, 8, 256), "wv": (2, 8, 512), "wf": (1, 8, 512),
    "wg": (8, 8, 256), "wfa": (8, 12, 128), "wout": (2, 8, 512), "wup": (8, 8, 512),
    "wdown": (8, 4, 1024), "wple": (1, 2, 1024), "wpg": (2, 8, 512),
}


def build(cfg):
    KSTOP = cfg.get('stop', '')
    NQ = cfg["NQ"]
    units = cfg["units"]
    SMAX = max(u["S"] for u in units)
    NU = len(units)
    NTOK = NU * NQ
    nc = bass.Bass("TRN2", target_bir_lowering=False)
    P = Prog(nc)

    def din(name, shape, dt=F32):
        return nc.dram_tensor(name, list(shape), dt, kind="ExternalInput").ap()

    def dscr(name, shape, dt=BF16):
        return nc.dram_tensor(name, list(shape), dt, kind="Internal").ap()

    dr = {}
    dfts = {}
    for u, un in enumerate(units):
        S = un["S"]
        dr["xc%d" % u] = din("xc%d" % u, [S, D])
        if not un["same"]:
            dr["xq%d" % u] = din("xq%d" % u, [NQ, D])
            dr["ropeq%d" % u] = din("ropeq%d" % u, [2, 128, NQ])
        dr["pq%d" % u] = din("pq%d" % u, [NQ, PLE])
        dr["y%d" % u] = nc.dram_tensor("y%d" % u, [NQ, D], F32, kind="ExternalOutput").ap()
        dn = un["dft"]
        if dn not in dfts:
            dfts[dn] = din(dn, [2, NQ // 512, S // 512, 128, 4, 512], BF16)
    ropec = din("ropec", [2, 128, SMAX])
    wsrc = {k: din(k, [v[0], 128, v[1], v[2]]) for k, v in WSPEC.items()}
    wb = {k: dscr("b_" + k, [v[0], 128, v[1], v[2]]) for k, v in WSPEC.items()}
    cdft_in = din("cdft", [128, 256])
    gpre_in = din("gpre", [128, 8])
    gmlp_in = din("gmlp", [128, 8])
    subln_in = din("subln", [128, 1])
    gpost_in = din("gpost", [3, D])
    lamv_in = din("lamv", [4, 64])
    kt_scr = dscr("kt_scr", [NH, 128, SMAX])
    v_scr = dscr("v_scr", [NH, SMAX // 128, 128, 128])
    g_scr = dscr("g_scr", [SMAX, 1024])
    x1_scr = dscr("x1_scr", [NTOK, D], F32)

    A = Arena(nc, 206 * 1024)
    ident = A.alloc([128, 128], BF16)
    identf = A.alloc([128, 128], F32)
    mh = A.alloc([128, 1], F32)
    gpre = A.alloc([128, 8], F32)
    gmlp = A.alloc([128, 8], F32)
    sub8 = A.alloc([128, 1], F32)
    nlam = A.alloc([128, 1], F32)
    gpost = A.alloc([128, 3, D], F32)
    cdft = A.alloc([128, 256], BF16)
    stat = A.alloc([128, 64], F32)
    Bconst = P.buf("const", dma=True, persist=True)
    Bstat = [P.buf("stat%d" % i) for i in range(16)]
    stat_i = [0]

    def stat4():
        i = stat_i[0]
        stat_i[0] = (i + 1) % 16
        return stat[:, 4 * i:4 * i + 4], Bstat[i]

    ps = nc.alloc_psum_tensor("ps", [128, 4096], F32)
    PB = [P.buf("pb%d" % i) for i in range(8)]

    def bank(i):
        return ps[:, i * 512:(i + 1) * 512]

    def bank_bf(i):
        return ps[:, i * 512:(i + 1) * 512].bitcast(BF16)

    rot = {"i": 0}

    def nextbank(pool=(0, 1, 2, 3, 4, 5, 6, 7)):
        i = pool[rot["i"] % len(pool)]
        rot["i"] += 1
        return i

    base_top = A.top

    def ld_const(e):
        r = [e.dma_start(out=gpre, in_=gpre_in), e.dma_start(out=gmlp, in_=gmlp_in),
             e.dma_start(out=sub8, in_=subln_in),
             e.dma_start(out=gpost[:, 0, :], in_=gpost_in[0:1, :].partition_broadcast(128)),
             e.dma_start(out=gpost[:, 1, :], in_=gpost_in[1:2, :].partition_broadcast(128)),
             e.dma_start(out=gpost[:, 2, :], in_=gpost_in[2:3, :].partition_broadcast(128))]
        return r
    P.dma("sp", ld_const, 6, Bconst, writes=[Bconst])
    Bcd = P.buf("cdft", dma=True, persist=True)
    P.dma("pool", lambda e: e.dma_start(out=cdft, in_=cdft_in), 1, Bcd, writes=[Bcd])
    Bid = P.buf("ident")

    Bmh = P.buf("mh")
    P.op("pool", lambda e: e.memset(mh, -0.5), writes=[Bmh])
    P.op("pool", lambda e: e.memset(identf, 0.0), writes=[Bid])
    P.op("pool", lambda e: e.affine_select(out=identf, in_=identf, pattern=[[-1, 128]], compare_op=ALU.not_equal,
                                           fill=1.0, base=0, channel_multiplier=1), reads=[Bid], writes=[Bid])
    P.op("dve", lambda e: e.tensor_copy(out=ident, in_=identf), reads=[Bid], writes=[Bid])
    P.op("dve", lambda e: e.tensor_scalar(out=sub8, in0=sub8, scalar1=1.0 - LAM_INIT, scalar2=None, op0=ALU.mult),
         reads=[Bconst], writes=[Bconst])
    m0 = A.top
    lamt = A.alloc([128, 4, 64], F32)
    lamp = A.alloc([128, 2, 64], F32)
    Blam = P.buf("lam", dma=True)
    P.dma("sp", lambda e: e.dma_start(out=lamt.rearrange("p a b -> p (a b)"),
                                      in_=lamv_in.rearrange("a b -> (a b)").partition_broadcast(128)),
          1, Blam, writes=[Blam])
    P.op("dve", lambda e: e.tensor_tensor(out=lamp[:, 0, :], in0=lamt[:, 0, :], in1=lamt[:, 1, :], op=ALU.mult),
         reads=[Blam], writes=[Blam])
    P.op("dve", lambda e: e.tensor_tensor(out=lamp[:, 1, :], in0=lamt[:, 2, :], in1=lamt[:, 3, :], op=ALU.mult),
         reads=[Blam], writes=[Blam])
    st0, Bst0 = stat4()
    P.op("dve", lambda e: e.reduce_sum(out=st0[:, 0:2], in_=lamp, axis=AX.X), reads=[Blam], writes=[Bst0])
    P.op("act", lambda e: e.activation(out=st0[:, 2:4], in_=st0[:, 0:2], func=AF.Exp), reads=[Bst0], writes=[Bst0])
    P.op("dve", lambda e: e.tensor_tensor(out=nlam, in0=st0[:, 3:4], in1=st0[:, 2:3], op=ALU.subtract),
         reads=[Bst0], writes=[Bconst])
    P.op("dve", lambda e: e.tensor_scalar(out=nlam, in0=nlam, scalar1=-LAM_INIT, scalar2=None, op0=ALU.add),
         reads=[Bconst], writes=[Bconst])

    WR = Ring(P, A, "wcv", 4, [128, 4096], BF16)
    Bw = {k: P.buf("wscr_" + k) for k in WSPEC}
    FIRST = ("wk2", "wv", "wf")
    for k in FIRST:
        nt, a, b = WSPEC[k]
        for t in range(nt):
            tl, tb, _ = WR.next()
            v = tl[:, 0:a * b]
            P.dma("pool", lambda e, v=v, k=k, t=t: e.dma_start(out=v, in_=wsrc[k][t].rearrange("p a b -> p (a b)")),
                  1, tb, writes=[tb])
            P.dma("sp", lambda e, v=v, k=k, t=t: e.dma_start(out=wb[k][t].rearrange("p a b -> p (a b)"), in_=v),
                  1, tb, reads=[tb], writes=[Bw[k]])
    P.barrier()
    P.release()
    A.top = m0

    hTq = A.alloc([128, 8, NQ], BF16)
    hq_top = A.top
    attT = A.alloc([128, 8, NQ], BF16)
    fmT = A.alloc([128, 4, NQ], BF16)
    BhTq = [P.buf("hTq%d" % i) for i in range(NQ // 512)]
    BattT = [P.buf("attT%d" % i) for i in range(NQ // 512)]
    BfmT = [P.buf("fmT%d" % i) for i in range(NQ // 512)]
    unit_top = A.top

    save_top = A.top
    bg_base = A.cap - 2 * 8192
    A.top = bg_base
    WR2 = Ring(P, A, "wcv2", 2, [128, 4096], BF16)
    A.top = save_top
    bg = []
    bgq = []
    for k, (nt, a, b) in WSPEC.items():
        if k in FIRST:
            continue
        for t in range(nt):
            bg.append((k, t, a * b))

    def bg_step():
        if bg:
            k, t, n = bg.pop(0)
            tl, tb, _ = WR2.next()
            v = tl[:, 0:n]
            P.dma("pool", lambda e, v=v, k=k, t=t: e.dma_start(out=v, in_=wsrc[k][t].rearrange("p a b -> p (a b)")),
                  1, tb, writes=[tb])
            bgq.append((k, t, v, tb))
        if len(bgq) > 1 or (not bg and bgq):
            k, t, v, tb = bgq.pop(0)
            P.dma("sp", lambda e, v=v, k=k, t=t: e.dma_start(out=wb[k][t].rearrange("p a b -> p (a b)"), in_=v),
                  1, tb, reads=[tb], writes=[Bw[k]])

    def bg_flush():
        while bg or bgq:
            bg_step()

    def wload(ring, k, t, cache=False):
        nt, a, b = WSPEC[k]
        tl, tb, hit = ring.next((k, t) if cache else None)
        v = tl[:, 0:a * b]
        if not hit:
            P.dma("sp", lambda e: e.dma_start(out=v, in_=wb[k][t].rearrange("p a b -> p (a b)")),
                  1, tb, reads=[Bw[k]], writes=[tb])
        return v.rearrange("p (a b) -> p a b", a=a), tb

    alt = {"i": 0}

    def evac_eng():
        alt["i"] += 1
        return "act" if alt["i"] % 2 else "dve"

    def copy_op(eng, out, in_, reads, writes, scale=None):
        if eng == "act":
            if scale is None:
                P.op("act", lambda e: e.activation(out=out, in_=in_, func=AF.Copy), reads=reads, writes=writes)
            else:
                P.op("act", lambda e: e.activation(out=out, in_=in_, func=AF.Copy, scale=scale), reads=reads, writes=writes)
        else:
            if scale is None:
                P.op("dve", lambda e: e.tensor_copy(out=out, in_=in_), reads=reads, writes=writes)
            else:
                P.op("dve", lambda e: e.tensor_scalar(out=out, in0=in_, scalar1=scale, scalar2=None, op0=ALU.mult),
                     reads=reads, writes=writes)

    def rstd_ops(st, Bst, extra_mul=None):
        P.op("pool", lambda e: e.tensor_scalar(out=st[:, 1:2], in0=st[:, 0:1], scalar1=EPS, scalar2=None, op0=ALU.add),
             reads=[Bst], writes=[Bst])
        P.op("pool", lambda e: e.tensor_tensor(out=st[:, 2:3], in0=st[:, 1:2], in1=mh, op=ALU.pow),
             reads=[Bst, Bmh], writes=[Bst])
        if extra_mul is not None:
            P.op("pool", lambda e: e.tensor_scalar(out=st[:, 2:3], in0=st[:, 2:3], scalar1=extra_mul, scalar2=None,
                                                   op0=ALU.mult), reads=[Bst], writes=[Bst])

    def norm_T_block(xt, Bx, gcol, dst, Bdst, xn4, Bxn4, junk, Bjunk, prenorm=True):
        for t in range(4):
            if prenorm:
                st, Bst = stat4()
                P.op("act", lambda e, t=t, st=st: e.activation(out=junk, in_=xt[:, t, :], func=AF.Square,
                                                               scale=1.0 / 32.0, accum_out=st[:, 0:1]),
                     reads=[Bx], writes=[Bjunk, Bst])
                rstd_ops(st, Bst)
                P.op("dve", lambda e, t=t, st=st: e.tensor_scalar(out=xn4[:, t, :], in0=xt[:, t, :], scalar1=st[:, 2:3],
                                                                  scalar2=None, op0=ALU.mult),
                     reads=[Bx, Bst], writes=[Bxn4[t]])
            else:
                P.op("dve", lambda e, t=t: e.tensor_copy(out=xn4[:, t, :], in_=xt[:, t, :]), reads=[Bx], writes=[Bxn4[t]])
        for kp in range(4):
            bi = nextbank()
            bv = bank_bf(bi)

            def tr(e, kp=kp, bv=bv):
                r = None
                for kk in range(2):
                    kc = kp * 2 + kk
                    for t in range(4):
                        r = e.transpose(out=bv[:, kk * 512 + t * 128: kk * 512 + (t + 1) * 128],
                                        in_=xn4[:, t, kc * 128:(kc + 1) * 128], identity=ident)
                return r
            P.op("pe", tr, reads=list(Bxn4) + [Bid], writes=[PB[bi]])
            for kk in range(2):
                kc = kp * 2 + kk
                copy_op(evac_eng(), dst[:, kc, :], bv[:, kk * 512:(kk + 1) * 512], [PB[bi], Bconst], [Bdst],
                        scale=(gcol[:, kc:kc + 1] if gcol is not None else None))

    def do_unit(u, un):
        S = un["S"]
        same = un["same"]
        xc = dr["xc%d" % u]
        xq = xc if same else dr["xq%d" % u]
        ropeq = ropec if same else dr["ropeq%d" % u]
        dft = dfts[un["dft"]]
        KCH = min(S, 2048)

        A.top = hq_top
        SB = min(S, NQ)
        nb = SB // 512
        XR = Ring(P, A, "xa", 2, [128, 4, 1024], F32)
        xn4s = [A.alloc([128, 4, 1024], BF16) for _ in range(2)]
        Bxn4s = [[P.buf("xn4_%d_%d" % (i, t)) for t in range(4)] for i in range(2)]
        xn4, Bxn4 = xn4s[0], Bxn4s[0]
        junk = A.alloc([128, 1024], BF16)
        Bjunk = P.buf("junk")
        W8 = Ring(P, A, "w8", 3, [128, 4096], BF16)
        W4 = Ring(P, A, "w4", 3, [128, 2048], BF16)
        MR = Ring(P, A, "mask", nb, [128, 2, 512], F32)
        ktst = [A.alloc([128, 512], BF16) for _ in range(3)]
        Bktst = [P.buf("ktst%d" % i, dma=True) for i in range(3)]
        tmpA = [A.alloc([128, 512], F32) for _ in range(2)]
        BtmpA = [P.buf("tmpA%d" % i) for i in range(2)]
        tmpB = [A.alloc([128, 512], F32) for _ in range(2)]
        BtmpB = [P.buf("tmpB%d" % i) for i in range(2)]
        vst = [A.alloc([128, 4, 1024], BF16) for _ in range(2)]
        Bvst = [P.buf("vst%d" % i, dma=True) for i in range(2)]
        fT = [A.alloc([128, 4, 512], BF16) for _ in range(2)]
        BfT = [P.buf("fT%d" % i) for i in range(2)]
        gst = [A.alloc([128, 1024], BF16) for _ in range(2)]
        Bgst = [P.buf("gst%d" % i, dma=True) for i in range(2)]
        Bkt = P.buf("kt_scr")
        Bv = P.buf("v_scr")
        Bg = P.buf("g_scr")
        ridx = [0]
        assert A.top <= bg_base, ("phase A arena overlaps background cast ring", A.top, bg_base)
        for sbk in range(S // SB):
            t0 = sbk * SB
            masks = []
            for b in range(nb):
                xt, Bx, _ = XR.next()
                P.dma("sp", lambda e, xt=xt, r0=t0 + b * 512: e.dma_start(
                    out=xt, in_=xc[r0:r0 + 512, :].rearrange("(t p) d -> p t d", p=128)), 1, Bx, writes=[Bx])
                mt, Bm, _ = MR.next()
                P.dma("sp", lambda e, mt=mt, r0=t0 + b * 512: [e.dma_start(out=mt[:, 0, :], in_=ropec[0, :, r0:r0 + 512]),
                                                               e.dma_start(out=mt[:, 1, :], in_=ropec[1, :, r0:r0 + 512])],
                      2, Bm, writes=[Bm])
                masks.append((mt, Bm))
                norm_T_block(xt, Bx, gpre, hTq[:, :, b * 512:(b + 1) * 512], BhTq[b], xn4s[b % 2], Bxn4s[b % 2], junk, Bjunk)
            for h in range(NH):
                w, bw = wload(W4, "wk2", h)
                for b in range(nb):
                    mt, Bm = masks[b]
                    ba = nextbank((0, 2, 4, 6))
                    bb = ba + 1

                    def mm(e, w=w, ba=ba, bb=bb, b=b):
                        r = None
                        for kc in range(8):
                            r = e.matmul(bank(ba), lhsT=w[:, kc, 0:128], rhs=hTq[:, kc, b * 512:(b + 1) * 512],
                                         start=(kc == 0), stop=(kc == 7))
                        for kc in range(8):
                            r = e.matmul(bank(bb), lhsT=w[:, kc, 128:256], rhs=hTq[:, kc, b * 512:(b + 1) * 512],
                                         start=(kc == 0), stop=(kc == 7))
                        return r
                    P.op("pe", mm, reads=[bw, BhTq[b]], writes=[PB[ba], PB[bb]])
                    i2 = ridx[0] % 2
                    i3 = ridx[0] % 3
                    ridx[0] += 1
                    ta, Bta, tb_, Btb = tmpA[i2], BtmpA[i2], tmpB[i2], BtmpB[i2]
                    dst, Bd = ktst[i3], Bktst[i3]
                    P.op("dve", lambda e, tb_=tb_, bb=bb, mt=mt: e.tensor_tensor(out=tb_, in0=bank(bb), in1=mt[:, 1, :], op=ALU.mult),
                         reads=[PB[bb], Bm], writes=[Btb])
                    P.op("dve", lambda e, ta=ta, ba=ba, mt=mt: e.tensor_tensor(out=ta, in0=bank(ba), in1=mt[:, 0, :], op=ALU.mult),
                         reads=[PB[ba], Bm], writes=[Bta])
                    P.op("pool", lambda e, ta=ta, tb_=tb_, dst=dst: e.tensor_tensor(out=dst, in0=ta, in1=tb_, op=ALU.add),
                         reads=[Bta, Btb], writes=[Bd])
                    P.dma("pool", lambda e, h=h, dst=dst, r0=t0 + b * 512: e.dma_start(out=kt_scr[h, :, r0:r0 + 512], in_=dst),
                          1, Bd, reads=[Bd], writes=[Bkt])
                    bg_step()
            wv = [wload(W8, "wv", i) for i in range(2)]
            for b in range(nb):
                vs = vst[b % 2]
                Bvs = Bvst[b % 2]
                for t in range(4):
                    for half in range(2):
                        w, bw = wv[half]
                        bi = nextbank()

                        def mm(e, w=w, bi=bi, c0=b * 512 + t * 128):
                            r = None
                            for kc in range(8):
                                r = e.matmul(bank(bi), lhsT=hTq[:, kc, c0:c0 + 128], rhs=w[:, kc, :],
                                             start=(kc == 0), stop=(kc == 7))
                            return r
                        P.op("pe", mm, reads=[bw, BhTq[b]], writes=[PB[bi]])
                        copy_op(evac_eng(), vs[:, t, half * 512:(half + 1) * 512], bank(bi), [PB[bi]], [Bvs])

                def vstore(e, vs=vs, tb=(t0 + b * 512) // 512):
                    r = []
                    for h in range(NH):
                        r.append(e.dma_start(out=v_scr[h, tb * 4:(tb + 1) * 4].rearrange("t p d -> p t d"),
                                             in_=vs[:, :, h * 128:(h + 1) * 128]))
                    return r
                P.dma("pool", vstore, NH, Bvs, reads=[Bvs], writes=[Bv])
                bg_step()
            wf, bwf = wload(W8, "wf", 0)
            for b in range(nb):
                fTb, BfTb = fT[b % 2], BfT[b % 2]
                for g in range(4):
                    bi = nextbank()

                    def mm(e, g=g, bi=bi, wf=wf, b=b):
                        r = None
                        for kc in range(8):
                            r = e.matmul(bank(bi), lhsT=wf[:, kc, g * 128:(g + 1) * 128], rhs=hTq[:, kc, b * 512:(b + 1) * 512],
                                         start=(kc == 0), stop=(kc == 7))
                        return r
                    P.op("pe", mm, reads=[bwf, BhTq[b]], writes=[PB[bi]])
                    copy_op(evac_eng(), fTb[:, g, :], bank(bi), [PB[bi]], [BfTb])
                for t in range(4):
                    b0 = nextbank((0, 2, 4, 6))
                    gs = gst[t % 2]
                    Bgs = Bgst[t % 2]

                    def mm(e, t=t, b0=b0, fTb=fTb):
                        r = None
                        for g in range(4):
                            r = e.matmul(ps[:, b0 * 512 + g * 256: b0 * 512 + (g + 1) * 256],
                                         lhsT=fTb[:, g, t * 128:(t + 1) * 128], rhs=cdft, start=True, stop=True)
                        return r
                    P.op("pe", mm, reads=[BfTb, Bcd], writes=[PB[b0], PB[b0 + 1]])
                    copy_op(evac_eng(), gs, ps[:, b0 * 512:b0 * 512 + 1024], [PB[b0], PB[b0 + 1]], [Bgs])
                    P.dma("pool", lambda e, gs=gs, r0=t0 + b * 512 + t * 128: e.dma_start(
                        out=g_scr[r0:r0 + 128, :], in_=gs), 1, Bgs, reads=[Bgs], writes=[Bg])
                    bg_step()
        bg_flush()
        P.barrier()

        if not same:
            for qb in range(NQ // 512):
                xt, Bx, _ = XR.next()
                P.dma("sp", lambda e, xt=xt, qb=qb: e.dma_start(
                    out=xt, in_=xq[qb * 512:(qb + 1) * 512, :].rearrange("(t p) d -> p t d", p=128)), 1, Bx, writes=[Bx])
                norm_T_block(xt, Bx, gpre, hTq[:, :, qb * 512:(qb + 1) * 512], BhTq[qb], xn4, Bxn4, junk, Bjunk)
            P.barrier()
        P.release()

        if KSTOP == 'A':
            return
        A.top = unit_top
        GR = Ring(P, A, "gr", 2, [128, 4, 1024], BF16)
        DR = Ring(P, A, "dr", 2, [128, 2, 2048], BF16)
        for sb in range(NQ // 512):
            nsg = S // 512
            for sg in range(nsg):
                gt, Bgt, _ = GR.next()
                P.dma("sp", lambda e, gt=gt, sg=sg: e.dma_start(
                    out=gt, in_=g_scr[sg * 512:(sg + 1) * 512, :].rearrange("(c p) f -> p c f", p=128)),
                    1, Bgt, reads=[Bg], writes=[Bgt])
                dt_, Bdt, _ = DR.next()
                P.dma("sp", lambda e, dt_=dt_, sb=sb, sg=sg: [
                    e.dma_start(out=dt_[:, 0, :], in_=dft[0, sb, sg].rearrange("p c f -> p (c f)")),
                    e.dma_start(out=dt_[:, 1, :], in_=dft[1, sb, sg].rearrange("p c f -> p (c f)"))],
                    2, Bdt, writes=[Bdt])

                def mm(e, gt=gt, dt_=dt_, sg=sg, nsg=nsg):
                    r = None
                    for g in range(4):
                        for sc in range(4):
                            for ri in range(2):
                                first = (sg == 0 and sc == 0 and ri == 0)
                                last = (sg == nsg - 1 and sc == 3 and ri == 1)
                                r = e.matmul(bank(g), lhsT=gt[:, sc, g * 256 + ri * 128: g * 256 + (ri + 1) * 128],
                                             rhs=dt_[:, ri, sc * 512:(sc + 1) * 512], start=first, stop=last)
                    return r
                P.op("pe", mm, reads=[Bgt, Bdt], writes=[PB[0], PB[1], PB[2], PB[3]])
            for g in range(4):
                copy_op(evac_eng(), fmT[:, g, sb * 512:(sb + 1) * 512], bank(g), [PB[g]], [BfmT[sb]])
        P.barrier()
        P.release()

        if KSTOP == 'F':
            return
        A.top = unit_top
        W4 = Ring(P, A, "w4b", 2, [128, 2048], BF16)
        MR = Ring(P, A, "maskq", 2, [128, 2, 256], F32)
        tmpA = [A.alloc([128, 256], F32) for _ in range(2)]
        BtmpA = [P.buf("tmpA%d" % i) for i in range(2)]
        tmpB = [A.alloc([128, 256], F32) for _ in range(2)]
        BtmpB = [P.buf("tmpB%d" % i) for i in range(2)]
        qT = [A.alloc([128, NQ], BF16) for _ in range(2)]
        BqT = [[P.buf("qT%d_%d" % (i, j)) for j in range(NQ // 512)] for i in range(2)]
        nkc = KCH // 128
        nck = S // KCH
        KR = Ring(P, A, "kr", 3, [128, KCH], BF16)
        VR = Ring(P, A, "vr", 3, [128, nkc, 130], BF16)
        for j in range(3):
            P.op("pool", lambda e, j=j: [e.memset(VR.tiles[j][:, :, 128:129], 1.0), e.memset(VR.tiles[j][:, :, 129:130], 0.0)],
                 writes=[VR.bufs[j]])
        PT = [A.alloc([128, 2, 512], BF16) for _ in range(3)]
        BPT = [P.buf("pt%d" % i) for i in range(3)]
        osb = [A.alloc([128, 8, 130], F32) for _ in range(2)]
        Bosb = [P.buf("osb%d" % i) for i in range(2)]
        fst = [A.alloc([128, 16], F32) for _ in range(2)]
        Bfst = [P.buf("fst%d" % i) for i in range(2)]
        mh4 = A.alloc([128, 4], F32)
        Bmh4 = P.buf("mh4")
        P.op("pool", lambda e: e.memset(mh4, -0.5), writes=[Bmh4])
        tq = A.alloc([128, 4, 128], F32)
        Btq = P.buf("tq")
        oq = A.alloc([128, 4, 128], F32)
        Boq = P.buf("oq")
        sq = A.alloc([128, 4, 128], F32)
        Bsq = P.buf("sq")
        attb = [A.alloc([128, 4, 128], BF16) for _ in range(2)]
        Battb = [P.buf("attb%d" % i) for i in range(2)]
        oslot = [(5, 0), (5, 130), (5, 260), (6, 0), (6, 130), (6, 260), (7, 0), (7, 130)]
        sc_pairs = [(0, 1), (2, 3)]
        stt = {"spi": 0, "pti": 0, "ridx": 0, "fin": 0}

        def emit_qpiece(hh, piece):
            c0 = piece * 256
            mt, Bm, _ = MR.next()
            P.dma("sp", lambda e, mt=mt, c0=c0: [e.dma_start(out=mt[:, 0, :], in_=ropeq[0, :, c0:c0 + 256]),
                                                 e.dma_start(out=mt[:, 1, :], in_=ropeq[1, :, c0:c0 + 256])],
                  2, Bm, writes=[Bm])
            w, bw = wload(W4, "wq2", hh, cache=True)

            def mm(e, w=w, c0=c0):
                r = None
                for kc in range(8):
                    r = e.matmul(ps[:, 2048:2304], lhsT=w[:, kc, 0:128], rhs=hTq[:, kc, c0:c0 + 256],
                                 start=(kc == 0), stop=(kc == 7))
                for kc in range(8):
                    r = e.matmul(ps[:, 2304:2560], lhsT=w[:, kc, 128:256], rhs=hTq[:, kc, c0:c0 + 256],
                                 start=False, stop=(kc == 7), skip_group_check=True)
                return r
            P.op("pe", mm, reads=[bw, BhTq[c0 // 512]], writes=[PB[4]])
            i2 = stt["ridx"] % 2
            stt["ridx"] += 1
            ta, Bta, tb_, Btb = tmpA[i2], BtmpA[i2], tmpB[i2], BtmpB[i2]
            P.op("dve", lambda e, tb_=tb_, mt=mt: e.tensor_tensor(out=tb_, in0=ps[:, 2304:2560], in1=mt[:, 1, :], op=ALU.mult),
                 reads=[PB[4], Bm], writes=[Btb])
            P.op("dve", lambda e, ta=ta, mt=mt: e.tensor_tensor(out=ta, in0=ps[:, 2048:2304], in1=mt[:, 0, :], op=ALU.mult),
                 reads=[PB[4], Bm], writes=[Bta])
            qdst = qT[hh % 2]
            P.op("dve", lambda e, ta=ta, tb_=tb_, qdst=qdst, c0=c0: e.tensor_tensor(out=qdst[:, c0:c0 + 256], in0=ta, in1=tb_, op=ALU.add),
                 reads=[Bta, Btb], writes=[BqT[hh % 2][c0 // 512]])

        def emit_transposes(item):
            ab, Bab, hh, qb = item
            bv = bank_bf(4)

            def tr(e, ab=ab, bv=bv):
                r = None
                for qs in range(4):
                    r = e.transpose(out=bv[:, qs * 128:(qs + 1) * 128], in_=ab[:, qs, :], identity=ident)
                return r
            P.op("pe", tr, reads=[Bab, Bid], writes=[PB[4]])
            copy_op("dve", attT[:, hh, qb * 512:(qb + 1) * 512], bv[:, 0:512], [PB[4], Bconst], [BattT[qb]], scale=sub8[:, 0:1])

        def emit_exp(item):
            b0, b1, vt, Bvt, kc, firstk = item
            pt = PT[stt["pti"] % 3]
            Bpt = BPT[stt["pti"] % 3]
            stt["pti"] += 1
            P.op("act", lambda e, pt=pt, b0=b0: e.activation(
                out=pt.rearrange("p a b -> p (a b)"), in_=ps[:, b0 * 512:b0 * 512 + 1024], func=AF.Exp, scale=0.125),
                reads=[PB[b0], PB[b1]], writes=[Bpt])
            return (pt, Bpt, vt, Bvt, kc, firstk)

        def emit_av(pitem):
            pt, Bpt, vt, Bvt, kc, firstk = pitem

            def avmm(e, pt=pt, vt=vt, kc=kc, firstk=firstk):
                r = None
                seen = set()
                for qs in range(4):
                    for c in range(2):
                        bk, off = oslot[qs * 2 + c]
                        st_ = firstk and (bk not in seen)
                        seen.add(bk)
                        r = e.matmul(ps[:, bk * 512 + off: bk * 512 + off + 130],
                                     lhsT=pt[:, c, qs * 128:(qs + 1) * 128], rhs=vt[:, kc, :],
                                     start=st_, stop=False, skip_group_check=True)
                return r
            P.op("pe", avmm, reads=[Bpt, Bvt], writes=[PB[5], PB[6], PB[7]])

        def getkv(hh, ck):
            kt, Bk, hit = KR.next((u, hh, ck))
            if not hit:
                P.dma("sp", lambda e, kt=kt, ck=ck, hh=hh: e.dma_start(out=kt, in_=kt_scr[hh, :, ck * KCH:(ck + 1) * KCH]),
                      1, Bk, reads=[Bkt], writes=[Bk])
            vt, Bvt, hit = VR.next((u, hh, ck))
            if not hit:
                P.dma("sp", lambda e, vt=vt, ck=ck, hh=hh: e.dma_start(
                    out=vt[:, :, 0:128], in_=v_scr[hh, ck * nkc:(ck + 1) * nkc].rearrange("c p d -> p c d")),
                    1, Bvt, reads=[Bv], writes=[Bvt])
            return kt, Bk, vt, Bvt

        def finalize(hh, qb):
            fi = stt["fin"] % 2
            stt["fin"] += 1
            ob, Bob, fs, Bfs, ab, Bab = osb[fi], Bosb[fi], fst[fi], Bfst[fi], attb[fi], Battb[fi]
            for bk, j0, n in ((5, 0, 3), (6, 3, 3), (7, 6, 2)):
                P.op("dve", lambda e, bk=bk, j0=j0, n=n, ob=ob: e.tensor_copy(
                    out=ob[:, j0:j0 + n, :], in_=ps[:, bk * 512: bk * 512 + n * 130].rearrange("p (a b) -> p a b", a=n)),
                    reads=[PB[bk]], writes=[Bob])
            ov = ob.rearrange("p (q c) d -> p q c d", c=2)
            P.op("dve", lambda e, ob=ob, fs=fs: e.reciprocal(out=fs[:, 0:8], in_=ob[:, :, 128]), reads=[Bob], writes=[Bfs])
            P.op("dve", lambda e, fs=fs: e.tensor_scalar(out=fs[:, 8:12], in0=fs[:, 1:8:2], scalar1=nlam[:, 0:1], scalar2=None, op0=ALU.mult),
                 reads=[Bfs, Bconst], writes=[Bfs])
            P.op("dve", lambda e, ov=ov, fs=fs: e.tensor_tensor(out=tq, in0=ov[:, :, 1, 0:128],
                                                                in1=fs[:, 8:12].unsqueeze(2).broadcast_to([128, 4, 128]), op=ALU.mult),
                 reads=[Bob, Bfs], writes=[Btq])
            P.op("dve", lambda e, ov=ov, fs=fs: e.tensor_tensor(out=oq, in0=ov[:, :, 0, 0:128],
                                                                in1=fs[:, 0:8:2].unsqueeze(2).broadcast_to([128, 4, 128]), op=ALU.mult),
                 reads=[Bob, Bfs], writes=[Boq])
            P.op("pool", lambda e: e.tensor_tensor(out=oq, in0=oq, in1=tq, op=ALU.add), reads=[Btq, Boq], writes=[Boq])
            P.op("pool", lambda e: e.tensor_tensor(out=sq, in0=oq, in1=oq, op=ALU.mult), reads=[Boq], writes=[Bsq])
            P.op("dve", lambda e, fs=fs: e.reduce_sum(out=fs[:, 12:16], in_=sq, axis=AX.X), reads=[Bsq], writes=[Bfs])
            P.op("pool", lambda e, fs=fs: e.tensor_scalar(out=fs[:, 12:16], in0=fs[:, 12:16], scalar1=1.0 / 128.0, scalar2=EPS,
                                                          op0=ALU.mult, op1=ALU.add), reads=[Bfs], writes=[Bfs])
            P.op("pool", lambda e, fs=fs: e.tensor_tensor(out=fs[:, 12:16], in0=fs[:, 12:16], in1=mh4, op=ALU.pow),
                 reads=[Bfs, Bmh4], writes=[Bfs])
            P.op("dve", lambda e, fs=fs, ab=ab: e.tensor_tensor(out=ab, in0=oq, in1=fs[:, 12:16].unsqueeze(2).broadcast_to([128, 4, 128]),
                                                                op=ALU.mult), reads=[Boq, Bfs], writes=[Bab])
            return (ab, Bab, hh, qb)

        npiece = NQ // 256
        for piece in range(npiece):
            emit_qpiece(0, piece)
        pending_tr = None
        inject_at = (5, 10)
        for h in range(NH):
            qTh = qT[h % 2]
            BqTh = BqT[h % 2]
            nextpieces = list(range(npiece)) if h + 1 < NH else []
            per_qb = (npiece + NQ // 512 - 1) // (NQ // 512)
            for qb in range(NQ // 512):
                todo = [nextpieces.pop(0) for _ in range(min(per_qb, len(nextpieces)))]
                its = [(ck, kc) for ck in range(nck) for kc in range(nkc)]
                nit = len(its)
                kvc = {}

                def emit_S(j, qb=qb, its=its, kvc=kvc):
                    ck, kc = its[j]
                    if ck not in kvc:
                        kvc[ck] = getkv(h, ck)
                        if ck + 1 < nck:
                            getkv(h, ck + 1)
                        elif qb + 1 < NQ // 512:
                            getkv(h, 0)
                        elif h + 1 < NH:
                            getkv(h + 1, 0)
                    kt, Bk, vt, Bvt = kvc[ck]
                    b0, b1 = sc_pairs[stt["spi"] % 2]
                    stt["spi"] += 1

                    def smm(e, kt=kt, kc=kc, b0=b0, b1=b1, qb=qb, qTh=qTh):
                        e.matmul(bank(b0), lhsT=kt[0:64, kc * 128:(kc + 1) * 128], rhs=qTh[0:64, qb * 512:(qb + 1) * 512],
                                 start=True, stop=True)
                        return e.matmul(bank(b1), lhsT=kt[64:128, kc * 128:(kc + 1) * 128],
                                        rhs=qTh[64:128, qb * 512:(qb + 1) * 512], start=True, stop=True)
                    P.op("pe", smm, reads=[Bk, BqTh[qb]], writes=[PB[b0], PB[b1]])
                    return (b0, b1, vt, Bvt, kc, j == 0)

                sitems = {}
                for j in range(min(2, nit)):
                    sitems[j] = emit_S(j)
                tr_at = min(8, nit - 1)
                q_at = (3, 11) if nit >= 14 else (1, nit - 1)
                for i in range(nit):
                    pitem = emit_exp(sitems.pop(i))
                    if i + 2 < nit:
                        sitems[i + 2] = emit_S(i + 2)
                    emit_av(pitem)
                    if i == tr_at and pending_tr is not None:
                        emit_transposes(pending_tr)
                        pending_tr = None
                    if i in q_at and todo:
                        emit_qpiece(h + 1, todo.pop(0))
                while todo:
                    emit_qpiece(h + 1, todo.pop(0))
                if pending_tr is not None:
                    emit_transposes(pending_tr)
                    pending_tr = None
                pending_tr = finalize(h, qb)
        if pending_tr is not None:
            emit_transposes(pending_tr)
        P.barrier()
        P.release()

        if KSTOP == 'ATT':
            return
        A.top = unit_top
        XR = Ring(P, A, "xt1", 2, [128, 4, 1024], F32)
        W4 = Ring(P, A, "w4c", 4, [128, 2048], BF16)
        W8 = Ring(P, A, "w8c", 2, [128, 4096], BF16)
        mT = A.alloc([128, 8, NQ], BF16)
        BmT = [P.buf("mT%d" % i) for i in range(NQ // 512)]
        tf = [A.alloc([128, 512], F32) for _ in range(2)]
        Btf = [P.buf("tf%d" % i) for i in range(2)]
        m1 = A.alloc([128, 512], F32)
        Bm1 = P.buf("m1")
        m2 = A.alloc([128, 512], F32)
        Bm2 = P.buf("m2")
        junkf = A.alloc([128, 1024], BF16)
        Bjunkf = P.buf("junkf")
        tmp1 = A.alloc([128, 1024], F32)
        Btmp1 = P.buf("tmp1")
        flip = 0
        for oc in range(8):
            wfa, bwfa = wload(W4, "wfa", oc)
            wg, bwg = wload(W4, "wg", oc)
            for qb in range(NQ // 512):
                cols = slice(qb * 512, (qb + 1) * 512)
                bf_, ba_, bzf, bza = (0, 1, 2, 3) if flip % 2 == 0 else (4, 5, 6, 7)
                flip += 1

                def mm(e, wfa=wfa, wg=wg, bf_=bf_, ba_=ba_, bzf=bzf, bza=bza, cols=cols):
                    r = None
                    for g in range(4):
                        r = e.matmul(bank(bf_), lhsT=wfa[:, g, :], rhs=fmT[:, g, cols], start=(g == 0), stop=(g == 3))
                    for hh in range(8):
                        r = e.matmul(bank(ba_), lhsT=wfa[:, 4 + hh, :], rhs=attT[:, hh, cols], start=(hh == 0), stop=(hh == 7))
                    for kc in range(8):
                        r = e.matmul(bank(bzf), lhsT=wg[:, kc, 0:128], rhs=hTq[:, kc, cols], start=(kc == 0), stop=(kc == 7))
                    for kc in range(8):
                        r = e.matmul(bank(bza), lhsT=wg[:, kc, 128:256], rhs=hTq[:, kc, cols], start=(kc == 0), stop=(kc == 7))
                    return r
                P.op("pe", mm, reads=[bwfa, bwg, BfmT[qb], BattT[qb], BhTq[qb]], writes=[PB[bf_], PB[ba_], PB[bzf], PB[bza]])
                P.op("act", lambda e, bzf=bzf: e.activation(out=tf[0], in_=bank(bzf), func=AF.Tanh, scale=0.5),
                     reads=[PB[bzf]], writes=[Btf[0]])
                P.op("act", lambda e, bza=bza: e.activation(out=tf[1], in_=bank(bza), func=AF.Tanh, scale=0.5),
                     reads=[PB[bza]], writes=[Btf[1]])
                P.op("dve", lambda e, bf_=bf_: e.scalar_tensor_tensor(out=m1, in0=tf[0], scalar=1.0, in1=bank(bf_),
                                                                      op0=ALU.add, op1=ALU.mult),
                     reads=[Btf[0], PB[bf_]], writes=[Bm1])
                P.op("dve", lambda e, ba_=ba_: e.scalar_tensor_tensor(out=m2, in0=tf[1], scalar=1.0, in1=bank(ba_),
                                                                      op0=ALU.add, op1=ALU.mult),
                     reads=[Btf[1], PB[ba_]], writes=[Bm2])
                P.op("dve", lambda e, oc=oc, cols=cols: e.tensor_tensor(out=mT[:, oc, cols], in0=m1, in1=m2, op=ALU.add),
                     reads=[Bm1, Bm2], writes=[BmT[qb]])
        wo = [wload(W8, "wout", i) for i in range(2)]
        for qb in range(NQ // 512):
            xt, Bx, _ = XR.next()
            P.dma("sp", lambda e, xt=xt, qb=qb: e.dma_start(
                out=xt, in_=xq[qb * 512:(qb + 1) * 512, :].rearrange("(t p) d -> p t d", p=128)), 1, Bx, writes=[Bx])
            for t in range(4):
                b0 = nextbank((0, 2, 4, 6))

                def mm(e, c0=qb * 512 + t * 128, b0=b0, wo=wo):
                    r = None
                    for half in range(2):
                        for oc in range(8):
                            r = e.matmul(bank(b0 + half), lhsT=mT[:, oc, c0:c0 + 128], rhs=wo[half][0][:, oc, :],
                                         start=(oc == 0), stop=(oc == 7))
                    return r
                P.op("pe", mm, reads=[BmT[qb], wo[0][1], wo[1][1]], writes=[PB[b0], PB[b0 + 1]])
                y = ps[:, b0 * 512: b0 * 512 + 1024]
                st, Bst = stat4()
                P.op("act", lambda e, y=y, st=st: e.activation(out=junkf, in_=y, func=AF.Square, scale=0.5 / 32.0,
                                                               accum_out=st[:, 0:1]), reads=[PB[b0], PB[b0 + 1]], writes=[Bjunkf, Bst])
                rstd_ops(st, Bst, extra_mul=0.5)
                P.op("dve", lambda e, y=y, st=st: e.scalar_tensor_tensor(out=tmp1, in0=y, scalar=st[:, 2:3], in1=gpost[:, 0, :],
                                                                         op0=ALU.mult, op1=ALU.mult),
                     reads=[PB[b0], PB[b0 + 1], Bst, Bconst], writes=[Btmp1])
                P.op("dve", lambda e, xt=xt, t=t: e.tensor_tensor(out=xt[:, t, :], in0=xt[:, t, :], in1=tmp1, op=ALU.add),
                     reads=[Btmp1, Bx], writes=[Bx])
            r0 = u * NQ + qb * 512
            Bx1 = P.buf("x1scr")
            un.setdefault("Bx1", []).append(Bx1)
            P.dma("act", lambda e, xt=xt, r0=r0: e.dma_start(out=x1_scr[r0:r0 + 512, :].rearrange("(t p) d -> p t d", p=128), in_=xt),
                  1, Bx, reads=[Bx], writes=[Bx1])
        P.barrier()
        P.release()

    for u_, un_ in enumerate(units):
        do_unit(u_, un_)

    if KSTOP:
        P.barrier()
        P.emit()
        return nc
    A.top = base_top
    wdn = A.alloc([128, 32, 1024], BF16)
    Bwdn = P.buf("wdn", dma=True)
    P.dma("sp", lambda e: [e.dma_start(out=wdn[:, 4 * t:4 * t + 4, :].rearrange("p a b -> p (a b)"),
                                       in_=wb["wdown"][t].rearrange("p a b -> p (a b)")) for t in range(8)],
          8, Bwdn, reads=[Bw["wdown"]], writes=[Bwdn])
    XR = Ring(P, A, "xm", 2, [128, 4, 1024], F32)
    PR = Ring(P, A, "pm", 1, [128, 4, 256], F32)
    W8 = Ring(P, A, "w8m", 3, [128, 4096], BF16)
    xn4 = A.alloc([128, 4, 1024], BF16)
    Bxn4 = [P.buf("xn4m_%d" % t) for t in range(4)]
    junk = A.alloc([128, 1024], BF16)
    Bjunk = P.buf("junkm")
    h2T = A.alloc([128, 8, 512], BF16)
    Bh2T = P.buf("h2T")
    x2T = h2T
    Bx2T = [Bh2T for t in range(4)]
    uT = A.alloc([128, 32, 512], BF16)
    BuT = [P.buf("uT%d" % i) for i in range(8)]
    rl = [A.alloc([128, 512], F32) for _ in range(2)]
    Brl = [P.buf("rl%d" % i) for i in range(2)]
    stg = [A.alloc([128, 1024], BF16) for _ in range(1)]
    Bstg = [P.buf("stg%d" % i) for i in range(1)]
    pbf = A.alloc([128, 4, 256], BF16)
    Bpbf = P.buf("pbf")
    pT = A.alloc([128, 2, 512], BF16)
    BpT = P.buf("pT")
    tg = A.alloc([128, 1024], F32)
    Btg = P.buf("tg")
    mst = {"rli": 0}
    blocks = [(u, qb) for u in range(NU) for qb in range(NQ // 512)]
    ADD_ENG = cfg.get("m_add_eng", "dve")

    def m_load(bi_):
        u, qb = blocks[bi_]
        r0 = u * NQ + qb * 512
        xt, Bx, _ = XR.next()
        P.dma("sp", lambda e, xt=xt, r0=r0: e.dma_start(
            out=xt, in_=x1_scr[r0:r0 + 512, :].rearrange("(t p) d -> p t d", p=128)), 1, Bx,
            reads=[units[u]["Bx1"][qb]], writes=[Bx])
        return xt, Bx

    def m_prenorm(xt, Bx):
        for t in range(4):
            st, Bst = stat4()
            P.op("act", lambda e, t=t, st=st: e.activation(out=junk, in_=xt[:, t, :], func=AF.Square,
                                                           scale=1.0 / 32.0, accum_out=st[:, 0:1]),
                 reads=[Bx], writes=[Bjunk, Bst])
            rstd_ops(st, Bst)
            P.op("dve", lambda e, t=t, st=st: e.tensor_scalar(out=xn4[:, t, :], in0=xt[:, t, :], scalar1=st[:, 2:3],
                                                              scalar2=None, op0=ALU.mult),
                 reads=[Bx, Bst], writes=[Bxn4[t]])

    def m_T():
        for kp in range(4):
            bi = nextbank((0, 1, 2, 3))
            bv = bank_bf(bi)

            def tr(e, kp=kp, bv=bv):
                r = None
                for kk in range(2):
                    kc = kp * 2 + kk
                    for t in range(4):
                        r = e.transpose(out=bv[:, kk * 512 + t * 128: kk * 512 + (t + 1) * 128],
                                        in_=xn4[:, t, kc * 128:(kc + 1) * 128], identity=ident)
                return r
            P.op("pe", tr, reads=list(Bxn4) + [Bid], writes=[PB[bi]])
            for kk in range(2):
                kc = kp * 2 + kk
                copy_op(evac_eng(), h2T[:, kc, :], bv[:, kk * 512:(kk + 1) * 512], [PB[bi], Bconst], [Bh2T],
                        scale=gmlp[:, kc:kc + 1])

    def m_up():
        for fg in range(8):
            w, bw = wload(W8, "wup", fg)
            for j in range(4):
                fc = fg * 4 + j
                bi = nextbank((0, 1, 2, 3))

                def mm(e, w=w, j=j, bi=bi):
                    r = None
                    for kc in range(8):
                        r = e.matmul(bank(bi), lhsT=w[:, kc, j * 128:(j + 1) * 128], rhs=h2T[:, kc, :],
                                     start=(kc == 0), stop=(kc == 7))
                    return r
                P.op("pe", mm, reads=[bw, Bh2T], writes=[PB[bi]])
                r_ = rl[mst["rli"] % 2]
                Br_ = Brl[mst["rli"] % 2]
                mst["rli"] += 1
                P.op("act", lambda e, r_=r_, bi=bi: e.activation(out=r_, in_=bank(bi), func=AF.Relu), reads=[PB[bi]], writes=[Br_])
                P.op("dve", lambda e, r_=r_, fc=fc: e.scalar_tensor_tensor(out=uT[:, fc, :], in0=r_, scalar=1.0, in1=r_,
                                                                           op0=ALU.mult, op1=ALU.mult),
                     reads=[Br_], writes=[BuT[fg]])

    def m_down(xt, Bx):
        for t in range(4):
            b0 = nextbank((4, 6))

            def mm(e, t=t, b0=b0):
                r = None
                for half in range(2):
                    for fc in range(32):
                        r = e.matmul(bank(b0 + half), lhsT=uT[:, fc, t * 128:(t + 1) * 128],
                                     rhs=wdn[:, fc, half * 512:(half + 1) * 512], start=(fc == 0), stop=(fc == 31))
                return r
            P.op("pe", mm, reads=list(BuT) + [Bwdn], writes=[PB[b0], PB[b0 + 1]])
            y = ps[:, b0 * 512: b0 * 512 + 1024]
            st, Bst = stat4()
            P.op("act", lambda e, y=y, st=st: e.activation(out=junk, in_=y, func=AF.Square, scale=1.0 / 32.0,
                                                           accum_out=st[:, 0:1]), reads=[PB[b0], PB[b0 + 1]], writes=[Bjunk, Bst])
            rstd_ops(st, Bst)
            P.op("dve", lambda e, y=y, st=st: e.scalar_tensor_tensor(out=tg, in0=y, scalar=st[:, 2:3], in1=gpost[:, 1, :],
                                                                     op0=ALU.mult, op1=ALU.mult),
                 reads=[PB[b0], PB[b0 + 1], Bst, Bconst], writes=[Btg])
            P.op(ADD_ENG, lambda e, xt=xt, t=t: e.tensor_tensor(out=xt[:, t, :], in0=xt[:, t, :], in1=tg, op=ALU.add),
                 reads=[Btg, Bx], writes=[Bx])

    def m_ple(bi_, xt, Bx):
        u, qb = blocks[bi_]
        ptile, Bp, _ = PR.next()
        P.dma("sp", lambda e, ptile=ptile, u=u, qb=qb: e.dma_start(
            out=ptile, in_=dr["pq%d" % u][qb * 512:(qb + 1) * 512, :].rearrange("(t p) d -> p t d", p=128)), 1, Bp, writes=[Bp])
        for t in range(4):
            sg, Bsg = stg[0], Bstg[0]
            P.op("dve", lambda e, t=t, sg=sg: e.tensor_copy(out=sg, in_=xt[:, t, :]), reads=[Bx], writes=[Bsg])
            bi = nextbank((0, 1, 2, 3))
            bv = bank_bf(bi)

            def tr(e, sg=sg, bv=bv):
                r = None
                for kc in range(8):
                    r = e.transpose(out=bv[:, kc * 128:(kc + 1) * 128], in_=sg[:, kc * 128:(kc + 1) * 128], identity=ident)
                return r
            P.op("pe", tr, reads=[Bsg, Bid], writes=[PB[bi]])
            copy_op(evac_eng(), x2T[:, :, t * 128:(t + 1) * 128], bv.rearrange("p (a b) -> p a b", a=8), [PB[bi]], [Bx2T[t]])
        P.op("dve", lambda e, ptile=ptile: e.tensor_copy(out=pbf, in_=ptile), reads=[Bp], writes=[Bpbf])
        bi = nextbank((0, 1, 2, 3))
        bv = bank_bf(bi)

        def trp(e, bv=bv):
            r = None
            for c in range(2):
                for t in range(4):
                    r = e.transpose(out=bv[:, c * 512 + t * 128: c * 512 + (t + 1) * 128],
                                    in_=pbf[:, t, c * 128:(c + 1) * 128], identity=ident)
            return r
        P.op("pe", trp, reads=[Bpbf, Bid], writes=[PB[bi]])
        copy_op("dve", pT.rearrange("p a b -> p (a b)"), bv, [PB[bi]], [BpT])
        wpgt = [wload(W8, "wpg", i) for i in range(2)]
        wple, Bwp = wload(W8, "wple", 0) if False else (None, None)
        for t in range(4):
            be = nextbank((0, 2))
            bg_ = nextbank((4, 6))

            def mm(e, t=t, be=be, bg_=bg_, wpgt=wpgt):
                r = None
                for half in range(2):
                    for c in range(2):
                        r = e.matmul(bank(be + half), lhsT=pT[:, c, t * 128:(t + 1) * 128],
                                     rhs=wple_r[:, c, half * 512:(half + 1) * 512], start=(c == 0), stop=(c == 1))
                for half in range(2):
                    for kc in range(8):
                        r = e.matmul(bank(bg_ + half), lhsT=x2T[:, kc, t * 128:(t + 1) * 128],
                                     rhs=wpgt[half][0][:, kc, :], start=(kc == 0), stop=(kc == 7))
                return r
            P.op("pe", mm, reads=[BpT, Bx2T[t], Bwple, wpgt[0][1], wpgt[1][1]], writes=[PB[be], PB[be + 1], PB[bg_], PB[bg_ + 1]])
            ye = ps[:, be * 512: be * 512 + 1024]
            yg = ps[:, bg_ * 512: bg_ * 512 + 1024]
            P.op("act", lambda e, yg=yg: e.activation(out=tg, in_=yg, func=AF.Tanh, scale=0.5),
                 reads=[PB[bg_], PB[bg_ + 1]], writes=[Btg])
            P.op("dve", lambda e, ye=ye: e.scalar_tensor_tensor(out=tg, in0=tg, scalar=1.0, in1=ye, op0=ALU.add, op1=ALU.mult),
                 reads=[PB[be], PB[be + 1]], writes=[Btg])
            st, Bst = stat4()
            P.op("act", lambda e, st=st: e.activation(out=junk, in_=tg, func=AF.Square, scale=0.5 / 32.0,
                                                      accum_out=st[:, 0:1]), reads=[Btg], writes=[Bjunk, Bst])
            rstd_ops(st, Bst, extra_mul=0.5)
            P.op("dve", lambda e, st=st: e.scalar_tensor_tensor(out=tg, in0=tg, scalar=st[:, 2:3], in1=gpost[:, 2, :],
                                                                op0=ALU.mult, op1=ALU.mult),
                 reads=[Bst, Bconst], writes=[Btg])
            P.op(ADD_ENG, lambda e, xt=xt, t=t: e.tensor_tensor(out=xt[:, t, :], in0=xt[:, t, :], in1=tg, op=ALU.add),
                 reads=[Btg, Bx], writes=[Bx])
        P.dma("act", lambda e, xt=xt, u=u, qb=qb: e.dma_start(
            out=dr["y%d" % u][qb * 512:(qb + 1) * 512, :].rearrange("(t p) d -> p t d", p=128), in_=xt),
            1, Bx, reads=[Bx])

    wple_r = A.alloc([128, 2, 1024], BF16)
    Bwple = P.buf("wple_r", dma=True)
    P.dma("sp", lambda e: e.dma_start(out=wple_r.rearrange("p a b -> p (a b)"), in_=wb["wple"][0].rearrange("p a b -> p (a b)")),
          1, Bwple, reads=[Bw["wple"]], writes=[Bwple])
    nblk = len(blocks)
    cur = m_load(0)
    m_prenorm(cur[0], cur[1])
    m_T()
    m_up()
    for bi_ in range(nblk):
        nxt = None
        if bi_ + 1 < nblk:
            nxt = m_load(bi_ + 1)
            m_prenorm(nxt[0], nxt[1])
        m_down(cur[0], cur[1])
        if nxt is not None:
            m_T()
            m_up()
        m_ple(bi_, *cur)
        cur = nxt
    P.barrier()
    P.emit()
    return nc


def _tile_w(W, cols):
    K, N = W.shape
    return np.ascontiguousarray(W.reshape(K // 128, 128, N // cols, cols).transpose(2, 1, 0, 3))


def _swap_cols(W):
    K, N = W.shape
    Wr = W.reshape(K, N // 64, 64).copy()
    out = Wr.copy()
    out[:, :, 0:8] = Wr[:, :, 8:16]
    out[:, :, 8:16] = Wr[:, :, 0:8]
    return out.reshape(K, N)


def _pair_tiles(Wa, Wb):
    ta = _tile_w(Wa, 128)
    tb = _tile_w(Wb, 128)
    return np.ascontiguousarray(np.concatenate([ta, tb], axis=3))


def prep_weights(w_in, w_fourier, w_attn, w_out, w_up, w_down, w_ple, w_ple_gate):
    o = 0
    wf = w_in[:, o:o + 512]; o += 512
    wq = w_in[:, o:o + 1024]; o += 1024
    wk = w_in[:, o:o + 1024]; o += 1024
    wv = w_in[:, o:o + 1024]; o += 1024
    wgf = w_in[:, o:o + 1024]; o += 1024
    wga = w_in[:, o:o + 1024]
    d = {}
    d["wq2"] = _pair_tiles(wq, _swap_cols(wq))
    d["wk2"] = _pair_tiles(wk, _swap_cols(wk))
    d["wv"] = _tile_w(wv, 512)
    d["wf"] = _tile_w(wf, 512)
    d["wg"] = _pair_tiles(wgf, wga)
    tf = _tile_w(w_fourier, 128)
    ta = _tile_w(w_attn, 128)
    d["wfa"] = np.ascontiguousarray(np.concatenate([tf, ta], axis=2))
    d["wout"] = _tile_w(w_out, 512)
    d["wup"] = _tile_w(w_up, 512)
    d["wdown"] = np.ascontiguousarray(w_down.reshape(8, 4, 128, 1024).transpose(0, 2, 1, 3))
    d["wple"] = np.ascontiguousarray(w_ple.reshape(1, 2, 128, 1024).transpose(0, 2, 1, 3))
    d["wpg"] = _tile_w(w_ple_gate, 512)
    return {k: v.astype(np.float32) for k, v in d.items()}


def rope_tables(pos):
    pos = np.asarray(pos, dtype=np.float32)
    inv_freq = (np.float32(ROPE_THETA) ** (-(np.arange(0, 16, 2, dtype=np.float32) / np.float32(16)))).astype(np.float32)
    ang = (pos[:, None] * inv_freq[None, :]).astype(np.float32)
    cos = np.cos(ang).astype(np.float32).T
    sin = np.sin(ang).astype(np.float32).T
    n = len(pos)
    cm = np.ones((128, n), np.float32)
    sm = np.zeros((128, n), np.float32)
    for c in range(2):
        cm[c * 64:c * 64 + 8] = cos
        cm[c * 64 + 8:c * 64 + 16] = cos
        sm[c * 64:c * 64 + 8] = -sin
        sm[c * 64 + 8:c * 64 + 16] = sin
    return np.stack([cm, sm])


def dft_tables(S, spos):
    spos = np.asarray(spos, dtype=np.int64)
    nq = len(spos)
    s = np.arange(S, dtype=np.int64)
    m = (s[:, None] * spos[None, :]) % S
    ang = m.astype(np.float64) * (2.0 * np.pi / S)
    sc = 1.0 / math.sqrt(128.0 * S)
    out = np.empty((2, nq // 512, S // 512, 128, 4, 512), dtype=ml_dtypes.bfloat16)
    for ri, f in enumerate((np.cos, np.sin)):
        t = (f(ang) * sc).astype(np.float32)
        t = t.reshape(S // 512, 4, 128, nq // 512, 512)
        out[ri] = t.transpose(3, 0, 2, 1, 4).astype(ml_dtypes.bfloat16)
    return out


def cdft_table():
    c = np.arange(128, dtype=np.int64)
    ang = ((c[:, None] * c[None, :]) % 128).astype(np.float64) * (2.0 * np.pi / 128)
    return np.concatenate([np.cos(ang), -np.sin(ang)], axis=1).astype(np.float32)


def small_inputs(norm_mix_pre, norm_mlp_pre, subln, norm_mix_post, norm_mlp_post, norm_ple_post,
                 lambda_q1, lambda_k1, lambda_q2, lambda_k2):
    d = {}
    d["gpre"] = np.ascontiguousarray(norm_mix_pre.reshape(8, 128).T).astype(np.float32)
    d["gmlp"] = np.ascontiguousarray(norm_mlp_pre.reshape(8, 128).T).astype(np.float32)
    d["subln"] = np.ascontiguousarray(subln.reshape(128, 1)).astype(np.float32)
    d["gpost"] = np.stack([norm_mix_post, norm_mlp_post, norm_ple_post]).astype(np.float32)
    d["lamv"] = np.stack([lambda_q1, lambda_k1, lambda_q2, lambda_k2]).astype(np.float32)
    d["cdft"] = cdft_table()
    return d


_CACHE = {}


def kernel(x_prompt, x_sample, p_prompt, p_sample, norm_mix_pre, w_in, w_fourier, w_attn, w_out,
           lambda_q1, lambda_k1, lambda_q2, lambda_k2, subln, norm_mix_post, norm_mlp_pre,
           w_up, w_down, norm_mlp_post, w_ple, w_ple_gate, norm_ple_post):
    f = lambda a: np.asarray(a, dtype=np.float32)
    x_prompt, x_sample, p_prompt, p_sample = f(x_prompt), f(x_sample), f(p_prompt), f(p_sample)
    NC = 8
    NQ = 2048
    SP, SS = 2048, 8192
    cfg = {"NQ": NQ, "units": [{"S": SP, "same": True, "dft": "dftp"} for _ in range(4)]
           + [{"S": SS, "same": False, "dft": "dfts"}]}
    nc = build(cfg)
    shared = prep_weights(f(w_in[0]), f(w_fourier[0]), f(w_attn[0]), f(w_out[0]), f(w_up[0]), f(w_down[0]),
                          f(w_ple[0]), f(w_ple_gate[0]))
    shared.update(small_inputs(f(norm_mix_pre[0]), f(norm_mlp_pre[0]), f(subln[0]), f(norm_mix_post[0]),
                               f(norm_mlp_post[0]), f(norm_ple_post[0]), f(lambda_q1[0]), f(lambda_k1[0]),
                               f(lambda_q2[0]), f(lambda_k2[0])))
    shared["ropec"] = rope_tables(np.arange(SS))
    shared["dftp"] = dft_tables(SP, np.arange(SP))
    in_maps = []
    for c in range(NC):
        m = dict(shared)
        for i in range(4):
            b = 4 * c + i
            m["xc%d" % i] = x_prompt[b]
            m["pq%d" % i] = p_prompt[0, b]
        sb, j = c // 4, c % 4
        qpos = np.arange(j * NQ, (j + 1) * NQ)
        m["xc4"] = x_sample[sb]
        m["xq4"] = np.ascontiguousarray(x_sample[sb, j * NQ:(j + 1) * NQ])
        m["pq4"] = np.ascontiguousarray(p_sample[0, sb, j * NQ:(j + 1) * NQ])
        m["ropeq4"] = rope_tables(qpos)
        m["dfts"] = dft_tables(SS, qpos)
        in_maps.append(m)
    res = run_bass_kernel_spmd(nc, in_maps, core_ids=list(range(NC)))
    y_prompt = np.empty((32, SP, D), np.float32)
    y_sample = np.empty((2, SS, D), np.float32)
    for c in range(NC):
        r = res.results[c]
        for i in range(4):
            y_prompt[4 * c + i] = r["y%d" % i]
        sb, j = c // 4, c % 4
        y_sample[sb, j * NQ:(j + 1) * NQ] = r["y4"]
    return (y_prompt, y_sample)
```
